# Optimizing a Trainium2 kernel written in Bass

```python
import jax, jax.numpy as jnp
from jax import lax
import numpy as np

D_MODEL = 1024
BATCH = 32
SEQ = 256
DEPTH = 1
DEC_BATCH = 8
DEC_SEQ = 1024
PAST_LEN = 256

GRID_W = 64
MIX_WIDTH = D_MODEL
CONV_WIDTH = MIX_WIDTH // 2
SSD_WIDTH = MIX_WIDTH - CONV_WIDTH
SSD_HEAD_DIM = 64
SSD_HEADS = SSD_WIDTH // SSD_HEAD_DIM
SSD_GROUPS = 2
HEADS_PER_GROUP = SSD_HEADS // SSD_GROUPS
D_STATE = 128
CHUNK = 128
KSIZE = 3
D_FF = 2816
N_MOD = 6
EPS = 1e-6
SSD_CONV_CH = SSD_WIDTH + 2 * SSD_GROUPS * D_STATE
D_IN_PROJ = 3 * CONV_WIDTH + 2 * SSD_WIDTH + 2 * SSD_GROUPS * D_STATE + 2 * SSD_HEADS

kernel_name = "hybrid_shortconv_ssd_convffn_diffusion_step"


def rmsnorm(x, g):
    xf = x.astype(jnp.float32)
    r = lax.rsqrt(jnp.mean(xf * xf, axis=-1, keepdims=True) + EPS)
    return (xf * r).astype(x.dtype) * g


def dwconv1d(x, w, b=None):
    L = x.shape[1]
    xp = jnp.pad(x, ((0, 0), (1, 1), (0, 0)))
    y = xp[:, 0:L] * w[0] + xp[:, 1:L + 1] * w[1] + xp[:, 2:L + 2] * w[2]
    return y if b is None else y + b


def dwconv2d_grid(x, w, b):
    bsz, L, ch = x.shape
    rows = L // GRID_W
    xg = jnp.pad(x.reshape(bsz, rows, GRID_W, ch), ((0, 0), (1, 1), (1, 1), (0, 0)))
    y = b
    for di in range(KSIZE):
        for dj in range(KSIZE):
            y = y + xg[:, di:di + rows, dj:dj + GRID_W] * w[di, dj]
    return y.reshape(bsz, L, ch)


def ssd_scan(x, dt, a, bm, cm, init_state):
    f32 = jnp.float32
    bsz, l = x.shape[0], x.shape[1]
    nc = l // CHUNK
    xdt = x.astype(f32) * dt[..., None]
    bh = jnp.repeat(bm.astype(f32), HEADS_PER_GROUP, axis=2)
    chh = jnp.repeat(cm.astype(f32), HEADS_PER_GROUP, axis=2)
    x_c = xdt.reshape(bsz, nc, CHUNK, SSD_HEADS, SSD_HEAD_DIM)
    b_c = bh.reshape(bsz, nc, CHUNK, SSD_HEADS, D_STATE)
    c_c = chh.reshape(bsz, nc, CHUNK, SSD_HEADS, D_STATE)
    da = (dt * a).reshape(bsz, nc, CHUNK, SSD_HEADS).transpose(0, 3, 1, 2)
    cs = jnp.cumsum(da, axis=-1)
    causal = jnp.tril(jnp.ones((CHUNK, CHUNK), dtype=bool))
    decay_in = jnp.exp(jnp.where(causal, cs[..., :, None] - cs[..., None, :], -jnp.inf))
    scores = jnp.einsum('bclhn,bcshn->bhcls', c_c, b_c) * decay_in
    y_diag = jnp.einsum('bhcls,bcshp->bclhp', scores, x_c)
    decay_to_end = jnp.exp(cs[..., -1:] - cs)
    chunk_states = jnp.einsum('bcshn,bhcs,bcshp->bchpn', b_c, decay_to_end, x_c)
    chunk_decay = jnp.exp(cs[..., -1])

    def step(state, inp):
        dec, add = inp
        return dec[..., None, None] * state + add, state

    final, starts = lax.scan(step, init_state.astype(f32),
                             (jnp.moveaxis(chunk_decay, 2, 0), jnp.moveaxis(chunk_states, 1, 0)))
    y_off = jnp.einsum('bclhn,cbhpn,bhcl->bclhp', c_c, starts, jnp.exp(cs))
    y = (y_diag + y_off).reshape(bsz, l, SSD_HEADS, SSD_HEAD_DIM)
    return y, final


def mixer(h, w_in, w_conv_short, w_conv_ssd, b_conv_ssd, dt_bias, a_log, d_skip,
          g_ssd_norm, w_out, init_f, init_b):
    bsz, l = h.shape[0], h.shape[1]
    proj = h @ w_in
    sizes = [CONV_WIDTH, CONV_WIDTH, CONV_WIDTH, SSD_WIDTH, SSD_CONV_CH, SSD_HEADS, SSD_HEADS]
    offs = [int(v) for v in np.cumsum(sizes)[:-1]]
    hc, gb, gc, z, xbc, dtf, dtb = jnp.split(proj, offs, axis=-1)
    out_a = gb * dwconv1d(gc * hc, w_conv_short)
    xbc = jax.nn.silu(dwconv1d(xbc, w_conv_ssd, b_conv_ssd))
    xs, bm, cm = jnp.split(xbc, [SSD_WIDTH, SSD_WIDTH + SSD_GROUPS * D_STATE], axis=-1)
    xh = xs.reshape(bsz, l, SSD_HEADS, SSD_HEAD_DIM)
    bm = bm.reshape(bsz, l, SSD_GROUPS, D_STATE)
    cm = cm.reshape(bsz, l, SSD_GROUPS, D_STATE)
    dt_f = jax.nn.softplus(dtf.astype(jnp.float32) + dt_bias[0].astype(jnp.float32))
    dt_b = jax.nn.softplus(dtb.astype(jnp.float32) + dt_bias[1].astype(jnp.float32))
    a = -jnp.exp(a_log.astype(jnp.float32))
    y_f, s_f = ssd_scan(xh, dt_f, a[0], bm, cm, init_f)
    y_b_rev, s_b = ssd_scan(xh[:, ::-1], dt_b[:, ::-1], a[1], bm[:, ::-1], cm[:, ::-1], init_b)
    y = y_f + y_b_rev[:, ::-1] + xh.astype(jnp.float32) * d_skip.astype(jnp.float32)[:, None]
    y = y.astype(h.dtype).reshape(bsz, l, SSD_WIDTH)
    y = rmsnorm(y * jax.nn.silu(z), g_ssd_norm)
    out = jnp.concatenate([out_a, y], axis=-1) @ w_out
    return out, s_f.astype(h.dtype), s_b.astype(h.dtype)


def layer(x, mod, is_latent, init_f, init_b, g_norm1, g_norm2, w_in, w_conv_short, w_conv_ssd,
          b_conv_ssd, dt_bias, a_log, d_skip, g_ssd_norm, w_out, w_up, w_ffn_conv, b_ffn_conv, w_down):
    shift_m, scale_m, gate_m = mod[:, :, 0], mod[:, :, 1], mod[:, :, 2]
    shift_f, scale_f, gate_f = mod[:, :, 3], mod[:, :, 4], mod[:, :, 5]
    h = rmsnorm(x, g_norm1) * (1.0 + scale_m) + shift_m
    mix, s_f, s_b = mixer(h, w_in, w_conv_short, w_conv_ssd, b_conv_ssd, dt_bias, a_log, d_skip,
                          g_ssd_norm, w_out, init_f, init_b)
    x = x + gate_m * mix
    h2 = rmsnorm(x, g_norm2) * (1.0 + scale_f) + shift_f
    u = h2 @ w_up
    if is_latent:
        u = dwconv2d_grid(u, w_ffn_conv, b_ffn_conv)
    else:
        u = dwconv1d(u, w_ffn_conv[1], b_ffn_conv)
    ug, uv = jnp.split(u, 2, axis=-1)
    x = x + gate_f * ((jax.nn.silu(ug) * uv) @ w_down)
    return x, s_f, s_b


def setup_inputs(seed: int = 0) -> dict:
    key = jax.random.key(seed)
    ks = jax.random.split(key, 32)
    f32 = jnp.float32
    nrm = lambda k, shp, s: jax.random.normal(k, shp, f32) * s
    dt0 = jnp.exp(jax.random.uniform(ks[14], (DEPTH, 2, SSD_HEADS), f32, np.log(1e-3), np.log(1e-1)))
    return {
        "x_prompt": nrm(ks[0], (BATCH, SEQ, D_MODEL), 1.0),
        "x_sample": nrm(ks[1], (DEC_BATCH, DEC_SEQ, D_MODEL), 1.0),
        "state_ssd_fwd": nrm(ks[2], (DEC_BATCH, DEPTH, SSD_HEADS, SSD_HEAD_DIM, D_STATE), 0.5),
        "state_ssd_bwd": nrm(ks[3], (DEC_BATCH, DEPTH, SSD_HEADS, SSD_HEAD_DIM, D_STATE), 0.5),
        "c": nrm(ks[4], (DEC_BATCH, D_MODEL), 1.0),
        "c_ctx": nrm(ks[5], (D_MODEL,), 1.0),
        "g_norm1": 1.0 + nrm(ks[6], (DEPTH, D_MODEL), 0.02),
        "g_norm2": 1.0 + nrm(ks[7], (DEPTH, D_MODEL), 0.02),
        "w_ada": nrm(ks[8], (DEPTH, D_MODEL, N_MOD * D_MODEL), 0.5 * D_MODEL ** -0.5),
        "b_ada": nrm(ks[9], (DEPTH, N_MOD * D_MODEL), 0.02),
        "w_in": nrm(ks[10], (DEPTH, D_MODEL, D_IN_PROJ), D_MODEL ** -0.5),
        "w_conv_short": nrm(ks[11], (DEPTH, KSIZE, CONV_WIDTH), KSIZE ** -0.5),
        "w_conv_ssd": nrm(ks[12], (DEPTH, KSIZE, SSD_CONV_CH), KSIZE ** -0.5),
        "b_conv_ssd": nrm(ks[13], (DEPTH, SSD_CONV_CH), 0.02),
        "dt_bias": jnp.log(jnp.expm1(dt0)),
        "a_log": jnp.log(jax.random.uniform(ks[15], (DEPTH, 2, SSD_HEADS), f32, 1.0, 16.0)),
        "d_skip": 1.0 + nrm(ks[16], (DEPTH, SSD_HEADS), 0.1),
        "g_ssd_norm": 1.0 + nrm(ks[17], (DEPTH, SSD_WIDTH), 0.02),
        "w_out": nrm(ks[18], (DEPTH, MIX_WIDTH, D_MODEL), MIX_WIDTH ** -0.5),
        "w_up": nrm(ks[19], (DEPTH, D_MODEL, 2 * D_FF), D_MODEL ** -0.5),
        "w_ffn_conv": nrm(ks[20], (DEPTH, KSIZE, KSIZE, 2 * D_FF), KSIZE ** -1.0),
        "b_ffn_conv": nrm(ks[21], (DEPTH, 2 * D_FF), 0.02),
        "w_down": nrm(ks[22], (DEPTH, D_FF, D_MODEL), D_FF ** -0.5),
        "g_final": 1.0 + nrm(ks[23], (D_MODEL,), 0.02),
    }


def reference(x_prompt, x_sample, state_ssd_fwd, state_ssd_bwd, c, c_ctx, g_norm1, g_norm2,
              w_ada, b_ada, w_in, w_conv_short, w_conv_ssd, b_conv_ssd, dt_bias, a_log, d_skip,
              g_ssd_norm, w_out, w_up, w_ffn_conv, b_ffn_conv, w_down, g_final):
    xp = x_prompt
    xs = x_sample
    bp = x_prompt.shape[0]
    zero_state = jnp.zeros((bp, SSD_HEADS, SSD_HEAD_DIM, D_STATE), x_prompt.dtype)
    new_f, new_b = [], []
    for i in range(DEPTH):
        mod_ctx = (jax.nn.silu(c_ctx)[None] @ w_ada[i] + b_ada[i]).reshape(1, 1, N_MOD, D_MODEL)
        mod_lat = (jax.nn.silu(c) @ w_ada[i] + b_ada[i]).reshape(c.shape[0], 1, N_MOD, D_MODEL)
        wl = (g_norm1[i], g_norm2[i], w_in[i], w_conv_short[i], w_conv_ssd[i], b_conv_ssd[i],
              dt_bias[i], a_log[i], d_skip[i], g_ssd_norm[i], w_out[i], w_up[i], w_ffn_conv[i],
              b_ffn_conv[i], w_down[i])
        xp, s_f, s_b = layer(xp, mod_ctx, False, zero_state, zero_state, *wl)
        new_f.append(s_f)
        new_b.append(s_b)
        xs, _, _ = layer(xs, mod_lat, True, state_ssd_fwd[:, i], state_ssd_bwd[:, i], *wl)
    y_prompt = rmsnorm(xp, g_final)
    y_sample = rmsnorm(xs, g_final)
    new_state_ssd_fwd = jnp.stack(new_f, axis=1)
    new_state_ssd_bwd = jnp.stack(new_b, axis=1)
    return (y_prompt, y_sample, new_state_ssd_fwd, new_state_ssd_bwd)
```

```python
import os
import numpy as np
from contextlib import ExitStack
import concourse.bass as bass
import concourse.mybir as mybir
from concourse.bass_utils import run_bass_kernel_spmd

F32 = mybir.dt.float32
BF16 = mybir.dt.bfloat16
ALU = mybir.AluOpType
AF = mybir.ActivationFunctionType

ENGS = ("tensor", "vector", "scalar", "gpsimd", "sync")
EPS = 1e-6
NCORES = 8
SSD_PATTERN = os.environ.get("MK_SSDPAT", "AABBBABBABBBB")
STRICT = os.environ.get("MK_STRICT", "1") == "1"

G1, G2, BSHM, BSCM, BSHF, BSCF, WCS, WCX, BCX, WFC, BFC, NPP = 0, 8, 16, 24, 32, 40, 48, 60, 84, 92, 488, 532
BGM, BGF, GFIN, GSSD, DTB, ALOG, DSK, NR = 0, 1024, 2048, 3072, 3584, 3600, 3616, 3624


class Buf:
    __slots__ = ("name", "w", "r", "dsem", "dcount")

    def __init__(self, name):
        self.name = name
        self.w = None
        self.r = []
        self.dsem = None
        self.dcount = 0


class Op:
    __slots__ = ("eng", "fn", "deps", "signal", "seq", "pos", "dma", "dbuf", "dseq")

    def __init__(self, eng, fn):
        self.eng = eng
        self.fn = fn
        self.deps = []
        self.signal = False
        self.seq = 0
        self.pos = 0
        self.dma = False
        self.dbuf = None
        self.dseq = 0


class _Rec:
    def __init__(self):
        self.calls = []

    def __getattr__(self, name):
        def f(*a, **kw):
            self.calls.append((name, a, kw))
            return self
        return f


class Sched:
    def __init__(self, nc):
        self.nc = nc
        self.ops = {e: [] for e in ENGS}
        self.dma_bufs = []
        self.final_dmas = []
        self.pending_bar = {e: None for e in ENGS}

    def _dep(self, op, d, force=False):
        if d is None or d is op:
            return
        if d.dma:
            op.deps.append(d)
            return
        if d.eng == op.eng and not force:
            if op.eng == "tensor":
                return
            if op.pos - d.pos >= 2 and not STRICT:
                return
        if d.eng == op.eng and force:
            return
        op.deps.append(d)

    def barrier(self):
        last = [self.ops[e][-1] for e in ("tensor", "vector", "scalar") if self.ops[e]]
        for o in reversed(self.ops["gpsimd"]):
            if not o.dma:
                last.append(o)
                break
        for e in ENGS:
            self.pending_bar[e] = list(last)

    def op(self, eng, fn, reads=(), writes=(), nobar=False):
        rec = _Rec()
        fn(rec)
        assert len(rec.calls) == 1
        o = Op(eng, rec.calls[0])
        o.pos = len(self.ops[eng])
        if not nobar and self.pending_bar[eng] is not None:
            for d in self.pending_bar[eng]:
                self._dep(o, d, force=True)
            self.pending_bar[eng] = None
        for b in reads:
            self._dep(o, b.w)
        for b in writes:
            self._dep(o, b.w)
            for r in b.r:
                if r.eng == eng and not r.dma and not (STRICT and eng != "tensor"):
                    continue
                self._dep(o, r)
        for b in reads:
            if not o.dma:
                b.r = [r for r in b.r if r.dma or r.eng != eng]
            b.r.append(o)
        for b in writes:
            b.w = o
            b.r = []
        latest = {}
        keep = []
        for d in o.deps:
            if d.dma:
                keep.append(d)
            elif d.eng not in latest or latest[d.eng].pos < d.pos:
                latest[d.eng] = d
        o.deps = keep + list(latest.values())
        self.ops[eng].append(o)
        return o

    def dma(self, eng, out_ap, in_ap, reads=(), writes=(), sembuf=None, final=False, nobar=False, **kw):
        def fn(e):
            return e.dma_start(out=out_ap, in_=in_ap, **kw)
        o = self.op(eng, fn, reads, writes, nobar=nobar)
        o.dma = True
        sb = sembuf or (writes[0] if writes else reads[0])
        o.dbuf = sb
        sb.dcount += 1
        o.dseq = 16 * sb.dcount
        if sb not in self.dma_bufs:
            self.dma_bufs.append(sb)
        if final:
            self.final_dmas.append(o)
        return o

    def emit(self, stack):
        nc = self.nc
        sems = {e: stack.enter_context(nc.semaphore("s_" + e)) for e in ENGS}
        for b in self.dma_bufs:
            b.dsem = stack.enter_context(nc.semaphore("d_" + b.name))
        for e in ENGS:
            for o in self.ops[e]:
                for d in o.deps:
                    if not d.dma:
                        d.signal = True
        self.stats = {e: (len(self.ops[e]), sum(1 for o in self.ops[e] if o.signal and not o.dma)) for e in ENGS}
        for e in ENGS:
            n = 0
            for o in self.ops[e]:
                if o.signal and not o.dma:
                    n += 1
                    o.seq = n
        block = stack.enter_context(nc.Block())

        def run(e, h):
            waited = {}
            for o in self.ops[e]:
                need = {}
                for d in o.deps:
                    if d.dma:
                        key, val = d.dbuf.dsem, d.dseq
                    else:
                        key, val = sems[d.eng], d.seq
                    if need.get(key, 0) < val:
                        need[key] = val
                for key, val in need.items():
                    if waited.get(key, 0) < val:
                        h.wait_ge(key, val)
                        waited[key] = val
                name_, a_, kw_ = o.fn
                ins = getattr(h, name_)(*a_, **kw_)
                if o.dma:
                    ins.then_inc(o.dbuf.dsem, 16)
                elif o.signal:
                    ins.then_inc(sems[e], 1)
            if e == "sync":
                done = set()
                for o in self.final_dmas:
                    if o.dbuf not in done:
                        done.add(o.dbuf)
                        h.wait_ge(o.dbuf.dsem, 16 * o.dbuf.dcount)

        @block.tensor
        def _(h):
            run("tensor", h)

        @block.vector
        def _(h):
            run("vector", h)

        @block.scalar
        def _(h):
            run("scalar", h)

        @block.gpsimd
        def _(h):
            run("gpsimd", h)

        @block.sync
        def _(h):
            run("sync", h)


def run_pattern(ga, gb, pattern):
    live = {"A": ga, "B": gb}
    for ch in pattern:
        g = live.get(ch)
        if g is not None:
            try:
                next(g)
            except StopIteration:
                live[ch] = None
    for ch in ("B", "A"):
        g = live.get(ch)
        if g is not None:
            for _ in g:
                pass


def run_pipe(gens, depth=2, stagger=2):
    active = []
    it = iter(gens)
    pending = True
    while True:
        while pending and len(active) < depth and (not active or active[-1][1] >= stagger):
            try:
                active.append([next(it), 0])
            except StopIteration:
                pending = False
        if not active:
            break
        for a in list(active):
            try:
                next(a[0])
                a[1] += 1
            except StopIteration:
                active.remove(a)


def build_nc():
    nc = bass.Bass("TRN2", target_bir_lowering=False)

    def din(name, shape):
        return nc.dram_tensor(name, list(shape), F32, kind="ExternalInput").ap()

    x_d = din("x", [2048, 1024])
    cT_d = din("cT", [128, 16])
    stf_d = din("stf", [512, 128])
    stb_d = din("stb", [512, 128])
    ppc_d = din("ppc", [128, NPP])
    rowc_d = din("rowc", [NR])
    ident_d = din("ident", [128, 128])
    masks_d = din("masks", [5, 128, 128])
    w_ada_d = din("w_ada", [1024, 6144])
    w_in_d = din("w_in", [1024, 3088])
    w_out_d = din("w_out", [1024, 1024])
    w_up_d = din("w_up", [1024, 5632])
    w_down_d = din("w_down", [2816, 1024])
    y_d = nc.dram_tensor("y", [2048, 1024], F32, kind="ExternalOutput").ap()
    nsf_d = nc.dram_tensor("nsf", [4, 512, 128], F32, kind="ExternalOutput").ap()
    nsb_d = nc.dram_tensor("nsb", [4, 512, 128], F32, kind="ExternalOutput").ap()

    st = ExitStack()
    S = Sched(nc)

    def sb(name, shape, dt=F32):
        return st.enter_context(nc.sbuf_tensor("sb_" + name, list(shape), dt))

    identf = sb("identf", [128, 128])
    identb = sb("identb", [128, 128], BF16)
    masks = sb("masks", [128, 5, 128])
    M_GT, M_LT, T_LE, T_GE, ONES = (masks[:, i, :] for i in range(5))
    ppc = sb("ppc", [128, NPP])
    rowc = sb("rowc", [128, NR - BGF * 2])
    RO = GFIN
    gfin_bc = rowc[:, GFIN - RO:GFIN - RO + 1024]
    gssd_bc = rowc[:, GSSD - RO:GSSD - RO + 512]
    dtb_bc = rowc[:, DTB - RO:DTB - RO + 16]
    alog_bc = rowc[:, ALOG - RO:ALOG - RO + 16]
    dsk_bc = rowc[:, DSK - RO:DSK - RO + 8]
    a_bc = sb("a_bc", [128, 16])
    modpp = sb("modpp", [128, 4, 8, 2])
    gsc = sb("gsc", [128, 2, 8, 2])
    gate_bc = sb("gate_bc", [128, 2, 2, 1024])
    scT = sb("scT", [128, 8, 2], BF16)
    wdt = sb("wdt", [128, 8, 16], BF16)
    small = sb("small", [128, 64])
    ring = sb("ring", [128, 4, 4096], BF16)
    stage = sb("stage", [128, 2, 1024])
    ARENA_W = 34816
    arena = sb("arena", [128, ARENA_W])
    ps = st.enter_context(nc.psum_tensor("ps", [128, 4096], F32))

    def av(off, dt, shape):
        n = 1
        for s_ in shape[1:]:
            n *= s_
        nb = n * (2 if dt == BF16 else 4)
        assert off % 4 == 0 and nb % 4 == 0 and off + nb <= ARENA_W * 4, (off, nb)
        a = arena[:, off // 4:(off + nb) // 4]
        if dt == BF16:
            a = a.bitcast(BF16)
        if len(shape) > 2:
            names = "abcdefg"[:len(shape) - 1]
            pat = "p (%s) -> p %s" % (" ".join(names), " ".join(names))
            a = a.rearrange(pat, **{n_: shape[1 + k_] for k_, n_ in enumerate(names[:-1])})
        return a

    def bank(i, n=1):
        return ps[:, i * 512:(i + n) * 512]

    KB = 1024
    bufs = {}

    def B(name):
        if name not in bufs:
            bufs[name] = Buf(name.replace(".", "_").replace("[", "_").replace("]", ""))
        return bufs[name]

    PB = [B("psb%d" % i) for i in range(8)]
    ring_b = [B("ring%d" % i) for i in range(4)]
    ring_n = [0]

    def ring_load(dram_ap, shape3):
        s = ring_n[0] % 4
        ring_n[0] += 1
        a, b_ = shape3
        view = ring[:, s, 0:a * b_].rearrange("p (a b) -> p a b", a=a)
        S.dma("gpsimd", view, dram_ap, writes=[ring_b[s]], nobar=True)
        return view, ring_b[s]

    def wblock(w_d, c0, ncols, kcs=8):
        return w_d[:, c0:c0 + ncols].rearrange("(c p) n -> p c n", p=128), (kcs, ncols)

    S.dma("sync", identf[:], ident_d, writes=[B("identf")])
    S.dma("sync", masks[:], masks_d.rearrange("m p n -> p m n"), writes=[B("masks")])
    S.dma("sync", ppc[:], ppc_d, writes=[B("ppc")])
    S.dma("sync", rowc[:], rowc_d[GFIN:NR].partition_broadcast(128), writes=[B("rowc")])
    cT = av(0, F32, [128, 16])
    S.dma("sync", cT, cT_d, writes=[B("cT")])
    bg_bc = av(1 * KB, F32, [128, 2, 1024])
    S.dma("sync", bg_bc[:, 0, :], rowc_d[BGM:BGM + 1024].partition_broadcast(128), writes=[B("bg0")])
    S.dma("sync", bg_bc[:, 1, :], rowc_d[BGF:BGF + 1024].partition_broadcast(128), writes=[B("bg1")])
    S.dma("gpsimd", wdt[:], w_in_d[:, 3072:3088].rearrange("(c p) n -> p c n", p=128), writes=[B("wdt")], nobar=True)
    S.op("vector", lambda e: e.tensor_copy(out=identb[:], in_=identf[:]), reads=[B("identf")], writes=[B("identb")])
    S.op("scalar", lambda e: e.activation(out=scT[:].rearrange("p a b -> p (a b)"), in_=cT, func=AF.Silu),
         reads=[B("cT")], writes=[B("scT")])
    scB = av(10 * KB, BF16, [128, 8, 2, 128])
    S.op("vector", lambda e: e.tensor_copy(out=scB, in_=scT[:].unsqueeze(3).to_broadcast([128, 8, 2, 128])),
         reads=[B("scT")], writes=[B("scB")])
    S.op("scalar", lambda e: e.activation(out=a_bc[:], in_=alog_bc, func=AF.Exp), reads=[B("rowc")], writes=[B("a_bc")])
    S.op("vector", lambda e: e.tensor_scalar(out=a_bc[:], in0=a_bc[:], scalar1=-1.0, scalar2=None, op0=ALU.mult),
         reads=[B("a_bc")], writes=[B("a_bc")])

    modps = bank(7)[:, 0:64].rearrange("p (a b) -> p a b", a=32)
    first_mod = [True]

    def ada_blocks(lo, hi):
        for blk in range(lo, hi):
            wv, wb = ring_load(*wblock(w_ada_d, blk * 512, 512))
            mod = blk // 2
            if mod in (2, 5):
                g = 0 if mod == 2 else 1
                for j in range(2):
                    pb = 4 + (blk * 2 + j) % 3
                    for kc in range(8):
                        S.op("tensor", lambda e, pb=pb, j=j, kc=kc, wv=wv: e.matmul(
                            out=bank(pb), lhsT=scB[:, kc, j, :], rhs=wv[:, kc, :], start=(kc == 0), stop=(kc == 7)),
                            reads=[B("scB"), wb], writes=[PB[pb]])
                    half = blk % 2
                    S.op("vector", lambda e, pb=pb, g=g, j=j, half=half: e.tensor_tensor(
                        out=gate_bc[:, g, j, half * 512:(half + 1) * 512], in0=bank(pb),
                        in1=bg_bc[:, g, half * 512:(half + 1) * 512], op=ALU.add),
                        reads=[PB[pb], B("bg%d" % g)], writes=[B("gate_bc")])
            else:
                kind = {0: 0, 1: 1, 3: 2, 4: 3}[mod]
                for m in range(4):
                    idx = kind * 8 + (blk % 2) * 4 + m
                    for kc in range(8):
                        fm = first_mod[0]
                        first_mod[0] = False
                        S.op("tensor", lambda e, idx=idx, kc=kc, m=m, wv=wv, fm=fm: e.matmul(
                            out=modps[:, idx, :], lhsT=wv[:, kc, m * 128:(m + 1) * 128], rhs=scT[:, kc, :],
                            start=fm, stop=(kc == 7), skip_group_check=True),
                            reads=[B("scT"), wb], writes=[PB[7]])

    def ada_mods(q):
        k0 = 2 * q
        S.op("vector", lambda e: e.tensor_tensor(
            out=modpp[:, k0:k0 + 2, :, :].rearrange("p a b c -> p (a b) c"), in0=modps[:, k0 * 8:(k0 + 2) * 8, :],
            in1=ppc[:, BSHM + k0 * 8:BSHM + (k0 + 2) * 8].unsqueeze(2).to_broadcast([128, 16, 2]), op=ALU.add),
            reads=[PB[7], B("ppc")], writes=[B("modpp%d" % q)])
        kind, goff = ((1, G1), (3, G2))[q]
        S.op("vector", lambda e: e.scalar_tensor_tensor(
            out=gsc[:, q, :, :], in0=modpp[:, kind, :, :], scalar=1.0,
            in1=ppc[:, goff:goff + 8].unsqueeze(2).to_broadcast([128, 8, 2]), op0=ALU.add, op1=ALU.mult),
            reads=[B("modpp%d" % q), B("ppc")], writes=[B("gsc%d" % q)])

    sm_n = [0]

    def small_slot():
        i = sm_n[0] % 32
        sm_n[0] += 1
        return small[:, 2 * i:2 * i + 1], small[:, 2 * i + 1:2 * i + 2], B("sm%d" % i)

    def rms_scale(src_ap, src_bufs, width, junk_ap, junk_buf):
        ss, rr, sbuf_ = small_slot()
        S.op("scalar", lambda e: e.activation(out=junk_ap, in_=src_ap, func=AF.Square, accum_out=ss),
             reads=src_bufs, writes=[junk_buf, sbuf_])
        S.op("scalar", lambda e: e.activation(out=rr, in_=ss, func=AF.Ln, scale=1.0 / width, bias=EPS),
             reads=[sbuf_], writes=[sbuf_])
        S.op("scalar", lambda e: e.activation(out=rr, in_=rr, func=AF.Exp, scale=-0.5), reads=[sbuf_], writes=[sbuf_])
        return rr, sbuf_

    A_X = 0
    A_HT = 32 * KB
    A_CAT = 48 * KB
    A_P = 64 * KB

    def norm_to_hT(tb, which, hT, src_tiles, scratch=0, driver=None):
        j = tb
        q = 0 if which == "m" else 1
        shk = 0 if which == "m" else 2
        junk = av(A_P + scratch + 0 * KB, BF16, [128, 1024])
        def tile_gen(ip):
            tiles = (2 * ip, 2 * ip + 1)
            srcs = [src_tiles(i) for i in tiles]
            rrs = [rms_scale(src_ap, [src_b], 1024, junk, B("junk")) for (src_ap, src_b) in srcs]
            yield
            xns = []
            for (i, (src_ap, src_b), (rr, rb)) in zip(tiles, srcs, rrs):
                xn = av(A_P + scratch + 2 * KB + (i % 2) * 2 * KB, BF16, [128, 1024])
                xnb = B("xn%d" % (i % 2))
                S.op("scalar", lambda e, xn=xn, src_ap=src_ap, rr=rr: e.activation(out=xn, in_=src_ap, func=AF.Copy, scale=rr),
                     reads=[src_b, rb], writes=[xnb])
                xns.append((xn, xnb))
            yield
            pb0 = 2 * (ip % 2)
            for t, (xn, xnb) in enumerate(xns):
                pT = bank(pb0 + t).bitcast(BF16)
                for kc in range(8):
                    S.op("tensor", lambda e, pT=pT, xn=xn, kc=kc: e.transpose(
                        out=pT[:, kc * 128:(kc + 1) * 128], in_=xn[:, kc * 128:(kc + 1) * 128], identity=identb[:]),
                        reads=[xnb, B("identb")], writes=[PB[pb0 + t]])
            yield
            pT2 = bank(pb0, 2).bitcast(BF16).rearrange("p (t f) -> p t f", t=2)
            for kc in range(8):
                S.op("vector", lambda e, kc=kc: e.tensor_scalar(
                    out=hT[:, kc, 2 * ip * 128:(2 * ip + 2) * 128].rearrange("p (t c) -> p t c", t=2),
                    in0=pT2[:, :, kc * 128:(kc + 1) * 128],
                    scalar1=gsc[:, q, kc, j:j + 1], scalar2=modpp[:, shk, kc, j:j + 1], op0=ALU.mult, op1=ALU.add),
                    reads=[PB[pb0], PB[pb0 + 1], B("gsc%d" % q), B("modpp%d" % q)], writes=[B("hT")])
        if driver is not None:
            driver(tile_gen)
        else:
            run_pipe((tile_gen(ip) for ip in range(4)), depth=2, stagger=2)

    def proj_fm(wv, wb, col, hT, pbase, rbuf):
        for half in range(2):
            for kc in range(8):
                S.op("tensor", lambda e, half=half, kc=kc: e.matmul(
                    out=bank(pbase + half), lhsT=wv[:, kc, col:col + 128], rhs=hT[:, kc, half * 512:(half + 1) * 512],
                    start=(kc == 0), stop=(kc == 7)),
                    reads=[wb, rbuf], writes=[PB[pbase + half]])

    def process_tb(tb):
        j = tb
        nseq, L = (1, 1024) if tb == 0 else (4, 256)
        LP = L + 2
        hT = av(A_HT, BF16, [128, 8, 1024])
        catT = av(A_CAT, BF16, [128, 8, 1024])
        X = av(A_X, F32, [128, 8, 1024])

        n1_off = 0 if tb == 0 else 44 * KB
        xring = av(A_P + n1_off + 8 * KB, F32, [128, 2, 1024])

        def src_x(i):
            xb = B("xring%d" % (i % 2))
            S.dma("sync", xring[:, i % 2, :], x_d[(tb * 8 + i) * 128:(tb * 8 + i + 1) * 128, :], writes=[xb])
            return xring[:, i % 2, :], xb
        if tb == 0:
            norm_to_hT(tb, "m", hT, src_x, scratch=n1_off)
        else:
            norm_to_hT(tb, "m", hT, src_x, scratch=n1_off, driver=n1_deferred.append)
        yield

        BT = av(A_P + 0 * KB, BF16, [128, 2, 1024])
        CT = av(A_P + 4 * KB, BF16, [128, 2, 1024])
        xtok = av(A_P + 8 * KB, BF16, [128, 8, 768])
        rawp = av(A_P + 20 * KB, F32, [128, 2, nseq * LP])
        accb = av(A_P + 30 * KB, F32, [128, 2, 1024])
        xsT = av(A_P + 55 * KB, BF16, [128, 4, 1024])
        for r in range(2):
            rp3 = rawp[:, r, :].rearrange("p (s l) -> p s l", s=nseq)
            S.op("vector", lambda e, rp3=rp3: e.memset(rp3[:, :, 0:1], 0.0), writes=[B("rawp%d" % r)])
            S.op("vector", lambda e, rp3=rp3: e.memset(rp3[:, :, LP - 1:LP], 0.0), writes=[B("rawp%d" % r)])
        wx = [ring_load(*wblock(w_in_d, 2048 + 512 * q, 512)) for q in range(2)]
        wz = ring_load(*wblock(w_in_d, 1536, 512))

        def xbc_proj(jc):
            wv, wb = wx[jc // 4]
            proj_fm(wv, wb, (jc % 4) * 128, hT, 2 * (jc % 4), B("hT"))

        def xbc_dst(jc):
            if jc < 4:
                return xsT[:, jc, :], B("xsT%d" % jc)
            if jc < 6:
                return BT[:, jc - 4, :], B("BT%d" % (jc - 4))
            return CT[:, jc - 6, :], B("CT")

        def xbc_silu(jc):
            dst, db = xbc_dst(jc)
            S.op("scalar", lambda e: e.activation(out=dst, in_=accb[:, jc % 2, :], func=AF.Silu),
                 reads=[B("acc%d" % (jc % 2))], writes=[db])
        xbc_proj(0)
        xbc_proj(1)
        for jc in range(8):
            r = jc % 2
            pbase = 2 * (jc % 4)
            rp3 = rawp[:, r, :].rearrange("p (s l) -> p s l", s=nseq)
            rb = B("rawp%d" % r)
            ab = B("acc%d" % r)
            acc3 = accb[:, r, :].rearrange("p (s l) -> p s l", s=nseq)
            S.op("scalar", lambda e: e.activation(
                out=rp3[:, :, 1:L + 1], in_=bank(pbase, 2).rearrange("p (s l) -> p s l", s=nseq), func=AF.Copy),
                reads=[PB[pbase], PB[pbase + 1]], writes=[rb])
            if jc + 2 < 8:
                xbc_proj(jc + 2)
            if jc >= 1:
                xbc_silu(jc - 1)
            S.op("vector", lambda e: e.tensor_scalar(
                out=acc3, in0=rp3[:, :, 1:L + 1], scalar1=ppc[:, WCX + jc * 3 + 1:WCX + jc * 3 + 2],
                scalar2=ppc[:, BCX + jc:BCX + jc + 1], op0=ALU.mult, op1=ALU.add),
                reads=[rb, B("ppc")], writes=[ab])
            for k in (0, 2):
                S.op("vector", lambda e, k=k: e.scalar_tensor_tensor(
                    out=acc3, in0=rp3[:, :, k:k + L], scalar=ppc[:, WCX + jc * 3 + k:WCX + jc * 3 + k + 1],
                    in1=acc3, op0=ALU.mult, op1=ALU.add),
                    reads=[rb, ab, B("ppc")], writes=[ab])
        xbc_silu(7)
        for jc in range(6):
            dst, db = xbc_dst(jc)
            tb_ = jc % 2
            pT = bank(tb_).bitcast(BF16)
            for i in range(8):
                S.op("tensor", lambda e, i=i: e.transpose(
                    out=pT[:, i * 128:(i + 1) * 128], in_=dst[:, i * 128:(i + 1) * 128], identity=identb[:]),
                    reads=[db, B("identb")], writes=[PB[tb_]])
            S.op("scalar", lambda e: e.activation(
                out=xtok[:, :, jc * 128:(jc + 1) * 128], in_=pT.rearrange("p (i c) -> p i c", i=8), func=AF.Copy),
                reads=[PB[tb_]], writes=[B("xtok")])

        dtp = bank(6)[:, 0:128].rearrange("p (i c) -> p i c", i=8)
        for i in range(8):
            for kc in range(8):
                S.op("tensor", lambda e, i=i, kc=kc: e.matmul(
                    out=dtp[:, i, :], lhsT=hT[:, kc, i * 128:(i + 1) * 128], rhs=wdt[:, kc, :],
                    start=(i == 0 and kc == 0), stop=(kc == 7), skip_group_check=True),
                    reads=[B("hT"), B("wdt")], writes=[PB[6]])
        dsm = av(A_P + 42 * KB, F32, [128, 6, 8, 16])
        xs_, tmp_, dt_, da_, dtdte_ = dsm[:, 0], dsm[:, 1], dsm[:, 2], dsm[:, 3], dsm[:, 4]
        S.op("vector", lambda e: e.tensor_tensor(out=xs_, in0=dtp, in1=dtb_bc.unsqueeze(1).to_broadcast([128, 8, 16]), op=ALU.add),
             reads=[PB[6], B("rowc")], writes=[B("dsm")])
        S.op("scalar", lambda e: e.activation(out=tmp_, in_=xs_, func=AF.Abs),
             reads=[B("dsm")], writes=[B("dsm")])
        S.op("scalar", lambda e: e.activation(out=tmp_, in_=tmp_, func=AF.Exp, scale=-1.0), reads=[B("dsm")], writes=[B("dsm")])
        S.op("scalar", lambda e: e.activation(out=tmp_, in_=tmp_, func=AF.Ln, bias=1.0), reads=[B("dsm")], writes=[B("dsm")])
        S.op("vector", lambda e: e.scalar_tensor_tensor(out=dt_, in0=xs_, scalar=0.0, in1=tmp_, op0=ALU.max, op1=ALU.add),
             reads=[B("dsm")], writes=[B("dsm")])
        S.op("vector", lambda e: e.tensor_tensor(out=da_, in0=dt_, in1=a_bc[:].unsqueeze(1).to_broadcast([128, 8, 16]), op=ALU.mult),
             reads=[B("dsm"), B("a_bc")], writes=[B("dsm")])
        csp = bank(7)[:, 0:384].rearrange("p (n f) -> p n f", n=6)
        specs = [(0, T_LE), (8, T_GE), (0, M_GT), (8, M_LT), (0, ONES), (8, ONES)]
        for n_, (c_, msk) in enumerate(specs):
            S.op("tensor", lambda e, c_=c_, msk=msk, n_=n_: e.matmul(
                out=csp[:, n_, :], lhsT=msk, rhs=da_[:, :, c_:c_ + 8],
                start=(n_ == 0), stop=True, skip_group_check=True),
                reads=[B("dsm"), B("masks")], writes=[PB[7]])
        expall = av(A_P + 45 * KB, F32, [128, 8, 48])
        S.op("scalar", lambda e: e.activation(
            out=expall.rearrange("p i (n h) -> p n i h", n=6), in_=csp.rearrange("p n (i h) -> p n i h", i=8), func=AF.Exp),
            reads=[PB[7]], writes=[B("expall")])
        S.op("vector", lambda e: e.tensor_tensor(out=dtdte_, in0=expall[:, :, 16:32], in1=dt_, op=ALU.mult),
             reads=[B("expall"), B("dsm")], writes=[B("dsm2")])

        S.barrier()
        T0 = A_X
        Lb = av(T0 + 0 * KB, F32, [128, 2, 8, 128])
        E32 = av(T0 + 8 * KB, F32, [128, 2, 512])
        Gm = av(T0 + 12 * KB, F32, [128, 2, 2, 128])
        scb = av(T0 + 14 * KB, BF16, [128, 2, 2, 8, 128])
        xw = av(T0 + 22 * KB, BF16, [128, 2, 3, 512])
        xdc = av(T0 + 28 * KB, BF16, [128, 2, 512])
        S32 = av(A_P + 38 * KB, F32, [128, 2, 512])
        startb = av(A_P + 47 * KB, BF16, [128, 8, 512])
        yv = av(A_P + 55 * KB, F32, [128, 512])
        sz = av(A_P + 57 * KB, F32, [128, 512])
        yg = av(A_P + 59 * KB, F32, [128, 512])
        ynt = av(A_P + 61 * KB, BF16, [128, 512])
        junk2 = av(A_P + 62 * KB, BF16, [128, 512])
        stin = av(A_P + 63 * KB, F32, [128, 4, 128])
        Sfb = av(A_P + 65 * KB, BF16, [128, 512])
        o12 = av(A_P + 66 * KB, F32, [128, 2, 512])
        wzv, wzb = wz
        wA = [ring_load(*wblock(w_in_d, 512 * q, 512)) for q in range(3)]
        PB0a, PB0b = PB[0], PB[7]

        def bc_h(ap8):
            return ap8.unsqueeze(2).to_broadcast([128, 8, 64])

        def x3(ap512):
            return ap512.rearrange("p (h q) -> p h q", h=8)

        def load_state(src_d, d):
            S.dma("sync", stin, src_d.rearrange("(q p) n -> p q n", p=128), writes=[B("stin")])
            for q in range(4):
                S.op("tensor", lambda e, q=q: e.transpose(out=bank(5)[:, q * 128:(q + 1) * 128], in_=stin[:, q, :], identity=identf[:]),
                     reads=[B("stin"), B("identf")], writes=[PB[5]])
            S.op("scalar", lambda e, d=d: e.activation(out=S32[:, d, :], in_=bank(5), func=AF.Copy),
                 reads=[PB[5]], writes=[B("S32_%d" % d)])

        def store_state(dst_d, d, slot):
            for q in range(4):
                S.op("tensor", lambda e, q=q, d=d: e.transpose(out=bank(5)[:, q * 128:(q + 1) * 128], in_=S32[:, d, q * 128:(q + 1) * 128], identity=identf[:]),
                     reads=[B("S32_%d" % d), B("identf")], writes=[PB[5]])
            sv = stage[:, slot, 0:512].rearrange("p (q n) -> p q n", q=4)
            S.op("scalar", lambda e, sv=sv: e.activation(out=sv, in_=bank(5).rearrange("p (q n) -> p q n", q=4), func=AF.Copy),
                 reads=[PB[5]], writes=[B("stage%d" % slot)])
            S.dma("sync", dst_d.rearrange("(q p) n -> p q n", p=128), sv, reads=[B("stage%d" % slot)], final=True)

        def state_update(c, d, zero):
            col = 8 * d
            xdec = xdc[:, d, :]
            S.op("gpsimd", lambda e: e.tensor_tensor(out=x3(xdec), in0=x3(xtok[:, c, 0:512]), in1=bc_h(dtdte_[:, c, col:col + 8]), op=ALU.mult),
                 reads=[B("xtok"), B("dsm2")], writes=[B("xdec%d" % d)])
            for g in range(2):
                S.op("tensor", lambda e, g=g: e.matmul(out=bank(6)[:, g * 256:(g + 1) * 256], lhsT=xtok[:, c, 512 + g * 128:512 + (g + 1) * 128],
                                                   rhs=xdec[:, g * 256:(g + 1) * 256], start=(g == 0), stop=True, skip_group_check=True),
                     reads=[B("xtok"), B("xdec%d" % d)], writes=[PB[6]])
            sbuf_ = B("S32_%d" % d)
            if zero:
                S.op("vector", lambda e: e.tensor_copy(out=S32[:, d, :], in_=bank(6)), reads=[PB[6]], writes=[sbuf_])
            else:
                S.op("gpsimd", lambda e: e.tensor_tensor(out=x3(S32[:, d, :]), in0=x3(S32[:, d, :]), in1=bc_h(expall[:, c, 32 + col:32 + col + 8]), op=ALU.mult),
                     reads=[sbuf_, B("expall")], writes=[sbuf_])
                S.op("vector", lambda e: e.tensor_tensor(out=S32[:, d, :], in0=S32[:, d, :], in1=bank(6), op=ALU.add),
                     reads=[sbuf_, PB[6]], writes=[sbuf_])

        def stage_a(c, sl):
            tok = slice(c * 128, (c + 1) * 128)
            for q_, sc_ap in enumerate((dt_[:, c, 0:8], dt_[:, c, 8:16], dsk_bc)):
                rds = [B("xtok"), B("dsm") if q_ < 2 else B("rowc")]
                S.op("gpsimd", lambda e, q_=q_, sc_ap=sc_ap: e.tensor_tensor(out=x3(xw[:, sl, q_, :]), in0=x3(xtok[:, c, 0:512]), in1=bc_h(sc_ap), op=ALU.mult),
                     reads=rds, writes=[B("xw%d_%d" % (sl, q_))])
            for g in range(2):
                S.op("tensor", lambda e, g=g: e.matmul(out=bank(0)[:, g * 128:(g + 1) * 128], lhsT=BT[:, g, tok], rhs=CT[:, g, tok], start=True, stop=True),
                     reads=[B("BT%d" % g), B("CT")], writes=[PB0a])
            yield
            for d, msk in ((0, T_LE), (1, T_GE)):
                S.op("vector", lambda e, d=d, msk=msk: e.tensor_tensor(
                    out=Gm[:, d, :, :], in0=bank(0)[:, 0:256].rearrange("p (g l) -> p g l", g=2),
                    in1=msk.unsqueeze(1).to_broadcast([128, 2, 128]), op=ALU.mult),
                    reads=[PB0a, B("masks")], writes=[B("Gm%d" % d)])
            for d, msk in ((0, T_LE), (1, T_GE)):
                S.op("vector", lambda e, d=d, msk=msk: e.tensor_tensor(
                    out=Lb[:, d, :, :], in0=msk.unsqueeze(1).to_broadcast([128, 8, 128]),
                    in1=da_[:, c, 8 * d:8 * d + 8].unsqueeze(2).to_broadcast([128, 8, 128]), op=ALU.mult),
                    reads=[B("masks"), B("dsm")], writes=[B("Lb%d" % d)])
            yield
            for d, tri in ((0, M_GT), (1, M_LT)):
                if d == 1:
                    yield
                for hq in range(2):
                    pb = 1 + hq
                    S.op("tensor", lambda e, d=d, tri=tri, pb=pb, hq=hq: e.matmul(
                        out=bank(pb), lhsT=tri, rhs=Lb[:, d, hq * 4:(hq + 1) * 4, :].rearrange("p h l -> p (h l)"),
                        start=True, stop=True),
                        reads=[B("Lb%d" % d), B("masks")], writes=[PB[pb]])
                    S.op("scalar", lambda e, pb=pb: e.activation(out=bank(pb), in_=bank(pb), func=AF.Exp),
                         reads=[PB[pb]], writes=[PB[pb]])
                    S.op("vector", lambda e, d=d, hq=hq, pb=pb: e.tensor_tensor(
                        out=scb[:, sl, d, hq * 4:(hq + 1) * 4, :], in0=bank(pb).rearrange("p (h l) -> p h l", h=4),
                        in1=Gm[:, d, hq, :].unsqueeze(1).to_broadcast([128, 4, 128]), op=ALU.mult),
                        reads=[PB[pb], B("Gm%d" % d)], writes=[B("scb%d_%d" % (sl, d))])

        def stage_b(c, sl, fzero, b_off_zero, do_update):
            tok = slice(c * 128, (c + 1) * 128)
            if not fzero:
                S.op("scalar", lambda e: e.activation(out=Sfb, in_=S32[:, 0, :], func=AF.Copy), reads=[B("S32_0")], writes=[B("Sfb")])
            if do_update:
                state_update(c, 0, fzero)
            yield
            n_mm = 0
            for d in range(2):
                for h in range(8):
                    S.op("tensor", lambda e, d=d, h=h, n_mm=n_mm: e.matmul(
                        out=bank(3)[:, h * 64:(h + 1) * 64], lhsT=scb[:, sl, d, h, :], rhs=xw[:, sl, d, h * 64:(h + 1) * 64],
                        start=(n_mm == 0), stop=False, skip_group_check=True),
                        reads=[B("scb%d_%d" % (sl, d)), B("xw%d_%d" % (sl, d))], writes=[PB[3]])
                    n_mm += 1
            S.op("tensor", lambda e: e.matmul(out=bank(3), lhsT=identb[:], rhs=xw[:, sl, 2, :], start=False, stop=True, skip_group_check=True),
                 reads=[B("identb"), B("xw%d_2" % sl)], writes=[PB[3]])
            yield
            terms = []
            if not fzero:
                for g in range(2):
                    S.op("tensor", lambda e, g=g: e.matmul(out=bank(4)[:, g * 256:(g + 1) * 256], lhsT=CT[:, g, tok], rhs=Sfb[:, g * 256:(g + 1) * 256],
                                                       start=(g == 0), stop=True, skip_group_check=True),
                         reads=[B("CT"), B("Sfb")], writes=[PB[4]])
                S.op("vector", lambda e: e.tensor_tensor(out=x3(o12[:, 0, :]), in0=x3(bank(4)), in1=bc_h(expall[:, c, 0:8]), op=ALU.mult),
                     reads=[PB[4], B("expall")], writes=[B("o1")])
                terms.append((o12[:, 0, :], B("o1")))
            if not b_off_zero:
                for g in range(2):
                    S.op("tensor", lambda e, g=g: e.matmul(out=bank(5)[:, g * 256:(g + 1) * 256], lhsT=CT[:, g, tok], rhs=startb[:, c, g * 256:(g + 1) * 256],
                                                       start=(g == 0), stop=True, skip_group_check=True),
                         reads=[B("CT"), B("startb")], writes=[PB[5]])
                S.op("vector", lambda e: e.tensor_tensor(out=x3(o12[:, 1, :]), in0=x3(bank(5)), in1=bc_h(expall[:, c, 8:16]), op=ALU.mult),
                     reads=[PB[5], B("expall")], writes=[B("o2")])
                terms.append((o12[:, 1, :], B("o2")))
            yield
            if len(terms) == 2:
                S.op("gpsimd", lambda e: e.tensor_tensor(out=o12[:, 0, :], in0=o12[:, 0, :], in1=o12[:, 1, :], op=ALU.add),
                     reads=[B("o1"), B("o2")], writes=[B("o1")])
                terms = [terms[0]]
            if terms:
                tap, tbuf = terms[0]
                S.op("vector", lambda e, tap=tap: e.tensor_tensor(out=yv, in0=tap, in1=bank(3), op=ALU.add),
                     reads=[tbuf, PB[3]], writes=[B("yv")])
            else:
                S.op("vector", lambda e: e.tensor_copy(out=yv, in_=bank(3)), reads=[PB[3]], writes=[B("yv")])
            yield
            for kc in range(8):
                S.op("tensor", lambda e, kc=kc: e.matmul(out=bank(7), lhsT=hT[:, kc, tok], rhs=wzv[:, kc, :], start=(kc == 0), stop=(kc == 7)),
                     reads=[B("hT"), wzb], writes=[PB[7]])
            S.op("scalar", lambda e: e.activation(out=bank(7), in_=bank(7), func=AF.Silu), reads=[PB[7]], writes=[PB[7]])
            yield
            S.op("vector", lambda e: e.tensor_tensor(out=yg, in0=yv, in1=bank(7), op=ALU.mult), reads=[B("yv"), PB[7]], writes=[B("yg")])
            yield
            rr, rb = rms_scale(yg, [B("yg")], 512, junk2, B("junk2"))
            yield
            S.op("vector", lambda e, rr=rr: e.scalar_tensor_tensor(out=ynt, in0=yg, scalar=rr, in1=gssd_bc, op0=ALU.mult, op1=ALU.mult),
                 reads=[B("yg"), rb, B("rowc")], writes=[B("ynt")])
            pT = bank(7).bitcast(BF16)[:, 0:512]
            for q in range(4):
                S.op("tensor", lambda e, q=q, pT=pT: e.transpose(out=pT[:, q * 128:(q + 1) * 128], in_=ynt[:, q * 128:(q + 1) * 128], identity=identb[:]),
                     reads=[B("ynt"), B("identb")], writes=[PB0b])
            yield
            S.op("scalar", lambda e, pT=pT: e.activation(out=catT[:, 4:8, tok], in_=pT.rearrange("p (q t) -> p q t", q=4), func=AF.Copy),
                 reads=[PB0b], writes=[B("catT")])

        CPS = L // 128
        has_init = (tb == 0)
        b_zero_at = {}

        def bwd_gen():
            for s in range(nseq):
                chunks = list(range(s * CPS, (s + 1) * CPS))
                if has_init:
                    load_state(stb_d, 1)
                bzero = not has_init
                for c in reversed(chunks):
                    b_zero_at[c] = bzero
                    if not bzero:
                        S.op("scalar", lambda e, c=c: e.activation(out=startb[:, c, :], in_=S32[:, 1, :], func=AF.Copy),
                             reads=[B("S32_1")], writes=[B("startb")])
                    state_update(c, 1, bzero)
                    bzero = False
                    yield
                if tb == 1:
                    store_state(nsb_d[s], 1, 0)
        run_pattern(stage_a(0, 0), bwd_gen(), "BABBABBABB")
        allc = list(range(8))
        fzero = True
        for idx, c in enumerate(allc):
            s = c // CPS
            first = (c % CPS == 0)
            last = (c % CPS == CPS - 1)
            if first:
                if has_init:
                    load_state(stf_d, 0)
                fzero = not has_init
            gb_ = stage_b(c, idx % 2, fzero, b_zero_at[c], do_update=not (tb == 0 and last))
            ga_ = stage_a(allc[idx + 1], (idx + 1) % 2) if idx + 1 < 8 else None
            run_pattern(ga_, gb_, SSD_PATTERN)
            fzero = False
            if last and tb == 1:
                store_state(nsf_d[s], 0, 1)

        S.barrier()
        for i in range(8):
            S.dma("sync", X[:, i, :], x_d[(tb * 8 + i) * 128:(tb * 8 + i + 1) * 128, :], writes=[B("X%d" % i)])
        tpad = av(A_P + 20 * KB, F32, [128, 2, nseq * LP])
        acc2 = av(A_P + 30 * KB, F32, [128, 2, 1024])
        for r in range(2):
            rp3 = tpad[:, r, :].rearrange("p (s l) -> p s l", s=nseq)
            S.op("vector", lambda e, rp3=rp3: e.memset(rp3[:, :, 0:1], 0.0), writes=[B("rawp%d" % r)])
            S.op("vector", lambda e, rp3=rp3: e.memset(rp3[:, :, LP - 1:LP], 0.0), writes=[B("rawp%d" % r)])

        def mixa_proj(jc, which):
            widx = (0, 2, 1)[which]
            proj_fm(wA[widx][0], wA[widx][1], jc * 128, hT, 2 * ((3 * jc + which) % 4), B("hT"))

        def mixa_pb(jc, which):
            return 2 * ((3 * jc + which) % 4)
        for w_ in range(3):
            mixa_proj(0, w_)
        for jc in range(4):
            r = jc % 2
            tp3 = tpad[:, r, :].rearrange("p (s l) -> p s l", s=nseq)
            tb3 = B("rawp%d" % r)
            ab = B("acc%d" % r)
            a3 = acc2[:, r, :].rearrange("p (s l) -> p s l", s=nseq)
            ph, pg, pq = mixa_pb(jc, 0), mixa_pb(jc, 1), mixa_pb(jc, 2)
            S.op("scalar", lambda e: e.activation(out=acc2[:, r, :], in_=bank(ph, 2), func=AF.Copy),
                 reads=[PB[ph], PB[ph + 1]], writes=[ab])
            if jc + 1 < 4:
                mixa_proj(jc + 1, 0)
                mixa_proj(jc + 1, 1)
            S.op("vector", lambda e: e.tensor_tensor(
                out=tp3[:, :, 1:L + 1], in0=a3, in1=bank(pg, 2).rearrange("p (s l) -> p s l", s=nseq), op=ALU.mult),
                reads=[ab, PB[pg], PB[pg + 1]], writes=[tb3])
            if jc + 1 < 4:
                mixa_proj(jc + 1, 2)
            S.op("vector", lambda e: e.tensor_scalar(
                out=a3, in0=tp3[:, :, 0:L], scalar1=ppc[:, WCS + jc * 3:WCS + jc * 3 + 1], scalar2=None, op0=ALU.mult),
                reads=[tb3, B("ppc")], writes=[ab])
            for k in (1, 2):
                S.op("vector", lambda e, k=k: e.scalar_tensor_tensor(
                    out=a3, in0=tp3[:, :, k:k + L], scalar=ppc[:, WCS + jc * 3 + k:WCS + jc * 3 + k + 1], in1=a3, op0=ALU.mult, op1=ALU.add),
                    reads=[tb3, ab, B("ppc")], writes=[ab])
            S.op("vector", lambda e: e.tensor_tensor(out=catT[:, jc, :], in0=acc2[:, r, :], in1=bank(pq, 2), op=ALU.mult),
                 reads=[ab, PB[pq], PB[pq + 1]], writes=[B("catT")])
        S.barrier()

        wo = [ring_load(*wblock(w_out_d, 512 * q, 512)) for q in range(2)]
        tmpo = av(A_P + 0 * KB, F32, [128, 2, 512])

        def wout_tile(i):
            for half in range(2):
                pb = 4 + (i * 2 + half) % 4
                wv, wb = wo[half]
                for kc in range(8):
                    S.op("tensor", lambda e, kc=kc: e.matmul(
                        out=bank(pb), lhsT=catT[:, kc, i * 128:(i + 1) * 128], rhs=wv[:, kc, :], start=(kc == 0), stop=(kc == 7)),
                        reads=[B("catT"), wb], writes=[PB[pb]])
                t_ = tmpo[:, pb % 2, :]
                S.op("vector", lambda e: e.tensor_tensor(
                    out=t_, in0=bank(pb), in1=gate_bc[:, 0, j, half * 512:(half + 1) * 512], op=ALU.mult),
                    reads=[PB[pb], B("gate_bc")], writes=[B("tmpo%d" % (pb % 2))])
                S.op("gpsimd", lambda e: e.tensor_tensor(
                    out=X[:, i, half * 512:(half + 1) * 512], in0=X[:, i, half * 512:(half + 1) * 512], in1=t_, op=ALU.add),
                    reads=[B("X%d" % i), B("tmpo%d" % (pb % 2))], writes=[B("X%d" % i)])

        def wout_n2_driver(tile_gen):
            active = []
            for i in range(8):
                wout_tile(i)
                if i % 2 == 1:
                    active.append(tile_gen(i // 2))
                for g in list(active):
                    try:
                        next(g)
                    except StopIteration:
                        active.remove(g)
            while active:
                for g in list(active):
                    try:
                        next(g)
                    except StopIteration:
                        active.remove(g)
        norm_to_hT(tb, "f", hT, lambda i: (X[:, i, :], B("X%d" % i)), scratch=8 * KB, driver=wout_n2_driver)
        S.barrier()

        actT = av(A_P + 0 * KB, BF16, [128, 22, 1024])
        if tb == 0:
            RW = 18 * 66
            PE_TAPS = [(0, 0), (0, 1), (0, 2)]
        else:
            RW = 4 * 258
            PE_TAPS = []
        rawf = av(A_P + 44 * KB, F32, [128, 2, RW])
        accf = av(A_P + 54 * KB, F32, [128, 2, 1024])
        dg = av(A_P + 68 * KB, F32, [128, 2, 3, 128])
        for r in range(2):
            S.op("vector", lambda e, r=r: e.memset(rawf[:, r, :], 0.0), writes=[B("rawf%d" % r)])
        pend = [None]

        def flush_pend():
            if pend[0] is not None:
                pc, pr = pend[0]
                S.op("scalar", lambda e: e.activation(out=actT[:, pc, :], in_=accf[:, pr, :], func=AF.Silu),
                     reads=[B("accfD%d" % pr)], writes=[B("actT%d" % pc)])
                pend[0] = None
        upw = {}

        def u_base(c):
            return 2 * (c % 2) if PE_TAPS else 2 * (c % 4)

        def emit_proj(c):
            blk, m = divmod(c, 4)
            if blk not in upw:
                upw[blk] = ring_load(*wblock(w_up_d, 512 * blk, 512))
            if m == 0 and blk + 1 < 11 and (blk + 1) not in upw:
                upw[blk + 1] = ring_load(*wblock(w_up_d, 512 * (blk + 1), 512))
            wv, wb = upw[blk]
            proj_fm(wv, wb, m * 128, hT, u_base(c), B("hT"))
        emit_proj(0)
        for c in range(44):
            r = c % 2
            pbase = u_base(c)
            rb = B("rawf%d" % r)
            abD = B("accfD%d" % r)
            if tb == 0:
                r3 = rawf[:, r, :].rearrange("p (a b) -> p a b", a=18)
                a3 = accf[:, r, :].rearrange("p (a b) -> p a b", a=16)
                pin = bank(pbase, 2).rearrange("p (a b) -> p a b", a=16)
                ctr = r3[:, 1:17, 1:65]
                taps = [(di * 3 + dj, r3[:, di:di + 16, dj:dj + 64]) for di in range(3) for dj in range(3)
                        if (di, dj) != (1, 1) and (di, dj) not in PE_TAPS]
            else:
                r3 = rawf[:, r, :].rearrange("p (a b) -> p a b", a=4)
                a3 = accf[:, r, :].rearrange("p (a b) -> p a b", a=4)
                pin = bank(pbase, 2).rearrange("p (a b) -> p a b", a=4)
                ctr = r3[:, :, 1:257]
                taps = [(3 + k, r3[:, :, k:k + 256]) for k in (0, 2)]
            S.op("scalar", lambda e: e.activation(out=ctr, in_=pin, func=AF.Copy),
                 reads=[PB[pbase], PB[pbase + 1]], writes=[rb])
            if tb == 0 or c >= 22:
                S.op("scalar", lambda e: e.activation(
                    out=a3, in_=pin, func=AF.Identity, scale=ppc[:, WFC + c * 9 + 4:WFC + c * 9 + 5], bias=ppc[:, BFC + c:BFC + c + 1]),
                    reads=[PB[pbase], PB[pbase + 1], B("ppc")], writes=[abD])
            else:
                S.op("vector", lambda e: e.tensor_scalar(
                    out=a3, in0=ctr, scalar1=ppc[:, WFC + c * 9 + 4:WFC + c * 9 + 5], scalar2=ppc[:, BFC + c:BFC + c + 1],
                    op0=ALU.mult, op1=ALU.add),
                    reads=[rb, B("ppc")], writes=[abD])
            for ki, (di, dj) in enumerate(PE_TAPS):
                wi = di * 3 + dj
                S.op("scalar", lambda e, ki=ki, wi=wi: e.activation(
                    out=dg[:, r, ki, :], in_=identf[:], func=AF.Copy, scale=ppc[:, WFC + c * 9 + wi:WFC + c * 9 + wi + 1]),
                    reads=[B("identf"), B("ppc")], writes=[B("dg%d" % r)])
            flush_pend()
            if c + 1 < 44:
                emit_proj(c + 1)
            if PE_TAPS:
                vb = 4 + 2 * (c % 2)
                for half in range(2):
                    for ki, (di, dj) in enumerate(PE_TAPS):
                        S.op("tensor", lambda e, half=half, ki=ki, di=di, dj=dj: e.matmul(
                            out=bank(vb + half), lhsT=dg[:, r, ki, :],
                            rhs=r3[:, half * 8 + di:half * 8 + di + 8, dj:dj + 64],
                            start=(ki == 0), stop=(ki == len(PE_TAPS) - 1)),
                            reads=[rb, B("dg%d" % r)], writes=[PB[vb + half]])
            for (wi, view) in taps:
                S.op("vector", lambda e, view=view, wi=wi: e.scalar_tensor_tensor(
                    out=a3, in0=view, scalar=ppc[:, WFC + c * 9 + wi:WFC + c * 9 + wi + 1], in1=a3, op0=ALU.mult, op1=ALU.add),
                    reads=[rb, abD, B("ppc")], writes=[abD])
            if PE_TAPS:
                S.op("vector", lambda e: e.tensor_tensor(out=accf[:, r, :], in0=accf[:, r, :], in1=bank(vb, 2), op=ALU.add),
                     reads=[abD, PB[vb], PB[vb + 1]], writes=[abD])
            if c < 22:
                pend[0] = (c, r)
            else:
                S.op("gpsimd", lambda e: e.tensor_tensor(out=actT[:, c - 22, :], in0=actT[:, c - 22, :], in1=accf[:, r, :], op=ALU.mult),
                     reads=[abD, B("actT%d" % (c - 22))], writes=[B("actT%d" % (c - 22))])
        flush_pend()
        yield

        tmpd = av(A_P + 62 * KB, F32, [128, 2, 512])
        actb = [B("actT%d" % c) for c in range(22)]
        for cb in range(8):
            wv, wb = ring_load(w_down_d[:, cb * 128:(cb + 1) * 128].rearrange("(c p) n -> p c n", p=128), (22, 128))
            for hb in range(2):
                pb = 4 + (cb * 2 + hb) % 4
                for i4 in range(4):
                    i = hb * 4 + i4
                    for kc in range(22):
                        S.op("tensor", lambda e, pb=pb, i4=i4, i=i, kc=kc, wv=wv: e.matmul(
                            out=bank(pb)[:, i4 * 128:(i4 + 1) * 128], lhsT=actT[:, kc, i * 128:(i + 1) * 128], rhs=wv[:, kc, :],
                            start=(i4 == 0 and kc == 0), stop=(kc == 21), skip_group_check=True),
                            reads=[actb[kc], wb], writes=[PB[pb]])
                t_ = tmpd[:, hb, :].rearrange("p (i n) -> p i n", i=4)
                tbuf = B("tmpd%d" % hb)
                S.op("vector", lambda e, pb=pb, t_=t_, cb=cb: e.tensor_tensor(
                    out=t_, in0=bank(pb).rearrange("p (i n) -> p i n", i=4),
                    in1=gate_bc[:, 1, j, cb * 128:(cb + 1) * 128].unsqueeze(1).to_broadcast([128, 4, 128]), op=ALU.mult),
                    reads=[PB[pb], B("gate_bc")], writes=[tbuf])
                xv = X[:, hb * 4:(hb + 1) * 4, cb * 128:(cb + 1) * 128]
                xbs = [B("X%d" % (hb * 4 + q)) for q in range(4)]
                S.op("vector", lambda e, xv=xv, t_=t_: e.tensor_tensor(out=xv, in0=xv, in1=t_, op=ALU.add),
                     reads=xbs + [tbuf], writes=xbs)
            for h_ in down_hooks:
                h_(cb)
        for h_ in down_hooks:
            h_(None)
        S.barrier()

        junk = av(A_P + 66 * KB, BF16, [128, 1024])
        for i in range(8):
            rr, rb = rms_scale(X[:, i, :], [B("X%d" % i)], 1024, junk, B("junkf"))
            slot = i % 2
            S.op("vector", lambda e, rr=rr, i=i, slot=slot: e.scalar_tensor_tensor(
                out=stage[:, slot, :], in0=X[:, i, :], scalar=rr, in1=gfin_bc, op0=ALU.mult, op1=ALU.mult),
                reads=[B("X%d" % i), rb, B("rowc")], writes=[B("stage%d" % slot)])
            S.dma("sync", y_d[(tb * 8 + i) * 128:(tb * 8 + i + 1) * 128, :], stage[:, slot, :], reads=[B("stage%d" % slot)], final=True)

    n1_deferred = []
    down_hooks = []

    def n1_hook(cb, _st={"act": []}):
        act = _st["act"]
        if cb is not None:
            if cb % 2 == 1:
                act.append(n1_deferred[0](cb // 2))
            rounds = 1
        else:
            rounds = 8
        for _ in range(rounds):
            for g_ in list(act):
                try:
                    next(g_)
                except StopIteration:
                    act.remove(g_)

    ada_blocks(0, 4)
    ada_mods(0)
    g0 = process_tb(0)
    g1 = process_tb(1)
    next(g0)
    ada_blocks(4, 12)
    ada_mods(1)
    S.barrier()
    next(g0)
    S.barrier()
    next(g1)
    down_hooks.append(n1_hook)
    for _ in g0:
        pass
    down_hooks.clear()
    for _ in g1:
        pass
    S.emit(st)
    if os.environ.get("MK_STATS"):
        print("ops/signals per engine:", S.stats)
    st.close()
    return nc


_NC_CACHE = {}


def _consts():
    k = np.arange(128)
    m_gt = (k[:, None] > k[None, :]).astype(np.float32)
    m_lt = (k[:, None] < k[None, :]).astype(np.float32)
    t_le = (k[:, None] <= k[None, :]).astype(np.float32)
    t_ge = (k[:, None] >= k[None, :]).astype(np.float32)
    ones = np.ones((128, 128), np.float32)
    return np.eye(128, dtype=np.float32), np.stack([m_gt, m_lt, t_le, t_ge, ones])


def kernel(x_prompt, x_sample, state_ssd_fwd, state_ssd_bwd, c, c_ctx, g_norm1, g_norm2, w_ada, b_ada, w_in,
           w_conv_short, w_conv_ssd, b_conv_ssd, dt_bias, a_log, d_skip, g_ssd_norm, w_out, w_up, w_ffn_conv,
           b_ffn_conv, w_down, g_final):
    f = lambda a: np.ascontiguousarray(np.asarray(a, dtype=np.float32))
    x_prompt, x_sample, c, c_ctx = f(x_prompt), f(x_sample), f(c), f(c_ctx)
    stf, stb = f(state_ssd_fwd), f(state_ssd_bwd)
    b_ada0 = f(b_ada)[0]

    def pp(v, nch):
        return f(v).reshape(nch, 128).T

    ppc = np.zeros((128, NPP), np.float32)
    ppc[:, G1:G1 + 8] = pp(f(g_norm1)[0], 8)
    ppc[:, G2:G2 + 8] = pp(f(g_norm2)[0], 8)
    ppc[:, BSHM:BSHM + 8] = pp(b_ada0[0:1024], 8)
    ppc[:, BSCM:BSCM + 8] = pp(b_ada0[1024:2048], 8)
    ppc[:, BSHF:BSHF + 8] = pp(b_ada0[3072:4096], 8)
    ppc[:, BSCF:BSCF + 8] = pp(b_ada0[4096:5120], 8)
    wcs = f(w_conv_short)[0]
    ppc[:, WCS:WCS + 12] = wcs.reshape(3, 4, 128).transpose(2, 1, 0).reshape(128, 12)
    wcx = f(w_conv_ssd)[0]
    ppc[:, WCX:WCX + 24] = wcx.reshape(3, 8, 128).transpose(2, 1, 0).reshape(128, 24)
    ppc[:, BCX:BCX + 8] = pp(f(b_conv_ssd)[0], 8)
    wfc = f(w_ffn_conv)[0].reshape(9, 5632)
    ppc[:, WFC:WFC + 396] = wfc.reshape(9, 44, 128).transpose(2, 1, 0).reshape(128, 396)
    ppc[:, BFC:BFC + 44] = pp(f(b_ffn_conv)[0], 44)
    rowc = np.concatenate([b_ada0[2048:3072], b_ada0[5120:6144], f(g_final), f(g_ssd_norm)[0], f(dt_bias)[0].reshape(16),
                           f(a_log)[0].reshape(16), f(d_skip)[0]]).astype(np.float32)
    assert rowc.shape[0] == NR
    ident, masks = _consts()
    shared = {"ppc": ppc, "rowc": rowc, "ident": ident, "masks": masks, "w_ada": f(w_ada)[0], "w_in": f(w_in)[0],
              "w_out": f(w_out)[0], "w_up": f(w_up)[0], "w_down": f(w_down)[0]}
    in_maps = []
    for b in range(NCORES):
        cv = np.stack([c[b], c_ctx])
        cT = np.ascontiguousarray(cv.reshape(2, 8, 128).transpose(2, 1, 0).reshape(128, 16))
        m = dict(shared)
        m["x"] = np.concatenate([x_sample[b], x_prompt[4 * b:4 * b + 4].reshape(1024, 1024)], axis=0)
        m["cT"] = cT
        m["stf"] = np.ascontiguousarray(stf[b, 0].reshape(512, 128))
        m["stb"] = np.ascontiguousarray(stb[b, 0].reshape(512, 128))
        in_maps.append(m)
    if _NC_CACHE.get("prep_only"):
        return in_maps
    if "nc" not in _NC_CACHE:
        _NC_CACHE["nc"] = build_nc()
    res = run_bass_kernel_spmd(_NC_CACHE["nc"], in_maps, core_ids=list(range(NCORES)))
    y_prompt = np.empty((32, 256, 1024), np.float32)
    y_sample = np.empty((8, 1024, 1024), np.float32)
    nsf = np.empty((32, 1, 8, 64, 128), np.float32)
    nsb = np.empty((32, 1, 8, 64, 128), np.float32)
    for b in range(NCORES):
        r = res.results[b]
        y = np.asarray(r["y"])
        y_sample[b] = y[0:1024]
        y_prompt[4 * b:4 * b + 4] = y[1024:2048].reshape(4, 256, 1024)
        nsf[4 * b:4 * b + 4, 0] = np.asarray(r["nsf"]).reshape(4, 8, 64, 128)
        nsb[4 * b:4 * b + 4, 0] = np.asarray(r["nsb"]).reshape(4, 8, 64, 128)
    return (y_prompt, y_sample, nsf, nsb)
```

```python
import os
import numpy as np
from contextlib import ExitStack
import concourse.bass as bass
import concourse.mybir as mybir
from concourse.bass_utils import run_bass_kernel_spmd

F32 = mybir.dt.float32
BF16 = mybir.dt.bfloat16
ALU = mybir.AluOpType
AF = mybir.ActivationFunctionType

ENGS = ("tensor", "vector", "scalar", "gpsimd", "sync")
EPS = 1e-6
NCORES = 8
SSD_PATTERN = os.environ.get("MK_SSDPAT", "AABBBABBBABBB")
STRICT = os.environ.get("MK_STRICT", "1") == "1"

G1, G2, BSHM, BSCM, BSHF, BSCF, WCS, WCX, BCX, WFC, BFC, NPP = 0, 8, 16, 24, 32, 40, 48, 60, 84, 92, 488, 532
BGM, BGF, GFIN, GSSD, DTB, ALOG, DSK, NR = 0, 1024, 2048, 3072, 3584, 3600, 3616, 3624


class Buf:
    __slots__ = ("name", "w", "r", "dsem", "dcount")

    def __init__(self, name):
        self.name = name
        self.w = None
        self.r = []
        self.dsem = None
        self.dcount = 0


class Op:
    __slots__ = ("eng", "fn", "deps", "signal", "seq", "pos", "dma", "dbuf", "dseq")

    def __init__(self, eng, fn):
        self.eng = eng
        self.fn = fn
        self.deps = []
        self.signal = False
        self.seq = 0
        self.pos = 0
        self.dma = False
        self.dbuf = None
        self.dseq = 0


class _Rec:
    def __init__(self):
        self.calls = []

    def __getattr__(self, name):
        def f(*a, **kw):
            self.calls.append((name, a, kw))
            return self
        return f


class Sched:
    def __init__(self, nc):
        self.nc = nc
        self.ops = {e: [] for e in ENGS}
        self.dma_bufs = []
        self.final_dmas = []
        self.pending_bar = {e: None for e in ENGS}

    def _dep(self, op, d, force=False):
        if d is None or d is op:
            return
        if d.dma:
            op.deps.append(d)
            return
        if d.eng == op.eng and not force:
            if op.eng == "tensor":
                return
            if op.pos - d.pos >= 2 and not STRICT:
                return
        if d.eng == op.eng and force:
            return
        op.deps.append(d)

    def barrier(self):
        last = [self.ops[e][-1] for e in ("tensor", "vector", "scalar") if self.ops[e]]
        for o in reversed(self.ops["gpsimd"]):
            if not o.dma:
                last.append(o)
                break
        for e in ENGS:
            self.pending_bar[e] = list(last)

    def op(self, eng, fn, reads=(), writes=(), nobar=False):
        rec = _Rec()
        fn(rec)
        assert len(rec.calls) == 1
        o = Op(eng, rec.calls[0])
        o.pos = len(self.ops[eng])
        if not nobar and self.pending_bar[eng] is not None:
            for d in self.pending_bar[eng]:
                self._dep(o, d, force=True)
            self.pending_bar[eng] = None
        for b in reads:
            self._dep(o, b.w)
        for b in writes:
            self._dep(o, b.w)
            for r in b.r:
                if r.eng == eng and not r.dma and not (STRICT and eng != "tensor"):
                    continue
                self._dep(o, r)
        for b in reads:
            if not o.dma:
                b.r = [r for r in b.r if r.dma or r.eng != eng]
            b.r.append(o)
        for b in writes:
            b.w = o
            b.r = []
        latest = {}
        keep = []
        for d in o.deps:
            if d.dma:
                keep.append(d)
            elif d.eng not in latest or latest[d.eng].pos < d.pos:
                latest[d.eng] = d
        o.deps = keep + list(latest.values())
        self.ops[eng].append(o)
        return o

    def dma(self, eng, out_ap, in_ap, reads=(), writes=(), sembuf=None, final=False, nobar=False, **kw):
        def fn(e):
            return e.dma_start(out=out_ap, in_=in_ap, **kw)
        o = self.op(eng, fn, reads, writes, nobar=nobar)
        o.dma = True
        sb = sembuf or (writes[0] if writes else reads[0])
        o.dbuf = sb
        sb.dcount += 1
        o.dseq = 16 * sb.dcount
        if sb not in self.dma_bufs:
            self.dma_bufs.append(sb)
        if final:
            self.final_dmas.append(o)
        return o

    def emit(self, stack):
        nc = self.nc
        sems = {e: stack.enter_context(nc.semaphore("s_" + e)) for e in ENGS}
        for b in self.dma_bufs:
            b.dsem = stack.enter_context(nc.semaphore("d_" + b.name))
        for e in ENGS:
            for o in self.ops[e]:
                for d in o.deps:
                    if not d.dma:
                        d.signal = True
        self.stats = {e: (len(self.ops[e]), sum(1 for o in self.ops[e] if o.signal and not o.dma)) for e in ENGS}
        for e in ENGS:
            n = 0
            for o in self.ops[e]:
                if o.signal and not o.dma:
                    n += 1
                    o.seq = n
        block = stack.enter_context(nc.Block())

        def run(e, h):
            waited = {}
            for o in self.ops[e]:
                need = {}
                for d in o.deps:
                    if d.dma:
                        key, val = d.dbuf.dsem, d.dseq
                    else:
                        key, val = sems[d.eng], d.seq
                    if need.get(key, 0) < val:
                        need[key] = val
                for key, val in need.items():
                    if waited.get(key, 0) < val:
                        h.wait_ge(key, val)
                        waited[key] = val
                name_, a_, kw_ = o.fn
                ins = getattr(h, name_)(*a_, **kw_)
                if o.dma:
                    ins.then_inc(o.dbuf.dsem, 16)
                elif o.signal:
                    ins.then_inc(sems[e], 1)
            if e == "sync":
                done = set()
                for o in self.final_dmas:
                    if o.dbuf not in done:
                        done.add(o.dbuf)
                        h.wait_ge(o.dbuf.dsem, 16 * o.dbuf.dcount)

        @block.tensor
        def _(h):
            run("tensor", h)

        @block.vector
        def _(h):
            run("vector", h)

        @block.scalar
        def _(h):
            run("scalar", h)

        @block.gpsimd
        def _(h):
            run("gpsimd", h)

        @block.sync
        def _(h):
            run("sync", h)


def run_pattern(ga, gb, pattern):
    live = {"A": ga, "B": gb}
    for ch in pattern:
        g = live.get(ch)
        if g is not None:
            try:
                next(g)
            except StopIteration:
                live[ch] = None
    for ch in ("B", "A"):
        g = live.get(ch)
        if g is not None:
            for _ in g:
                pass


def run_pipe(gens, depth=2, stagger=2):
    active = []
    it = iter(gens)
    pending = True
    while True:
        while pending and len(active) < depth and (not active or active[-1][1] >= stagger):
            try:
                active.append([next(it), 0])
            except StopIteration:
                pending = False
        if not active:
            break
        for a in list(active):
            try:
                next(a[0])
                a[1] += 1
            except StopIteration:
                active.remove(a)


def build_nc():
    nc = bass.Bass("TRN2", target_bir_lowering=False)

    def din(name, shape):
        return nc.dram_tensor(name, list(shape), F32, kind="ExternalInput").ap()

    x_d = din("x", [2048, 1024])
    cT_d = din("cT", [128, 16])
    stf_d = din("stf", [512, 128])
    stb_d = din("stb", [512, 128])
    ppc_d = din("ppc", [128, NPP])
    rowc_d = din("rowc", [NR])
    ident_d = din("ident", [128, 128])
    masks_d = din("masks", [5, 128, 128])
    w_ada_d = din("w_ada", [1024, 6144])
    w_in_d = din("w_in", [1024, 3088])
    w_out_d = din("w_out", [1024, 1024])
    w_up_d = din("w_up", [1024, 5632])
    w_down_d = din("w_down", [2816, 1024])
    y_d = nc.dram_tensor("y", [2048, 1024], F32, kind="ExternalOutput").ap()
    nsf_d = nc.dram_tensor("nsf", [4, 512, 128], F32, kind="ExternalOutput").ap()
    nsb_d = nc.dram_tensor("nsb", [4, 512, 128], F32, kind="ExternalOutput").ap()

    st = ExitStack()
    S = Sched(nc)

    def sb(name, shape, dt=F32):
        return st.enter_context(nc.sbuf_tensor("sb_" + name, list(shape), dt))

    identf = sb("identf", [128, 128])
    identb = sb("identb", [128, 128], BF16)
    masks = sb("masks", [128, 5, 128])
    M_GT, M_LT, T_LE, T_GE, ONES = (masks[:, i, :] for i in range(5))
    ppc = sb("ppc", [128, NPP])
    rowc = sb("rowc", [128, NR - BGF * 2])
    RO = GFIN
    gfin_bc = rowc[:, GFIN - RO:GFIN - RO + 1024]
    gssd_bc = rowc[:, GSSD - RO:GSSD - RO + 512]
    dtb_bc = rowc[:, DTB - RO:DTB - RO + 16]
    alog_bc = rowc[:, ALOG - RO:ALOG - RO + 16]
    dsk_bc = rowc[:, DSK - RO:DSK - RO + 8]
    a_bc = sb("a_bc", [128, 16])
    modpp = sb("modpp", [128, 4, 8, 2])
    gsc = sb("gsc", [128, 2, 8, 2])
    gate_bc = sb("gate_bc", [128, 2, 2, 1024])
    scT = sb("scT", [128, 8, 2], BF16)
    wdt = sb("wdt", [128, 8, 16], BF16)
    small = sb("small", [128, 64])
    ring = sb("ring", [128, 4, 4096], BF16)
    stage = sb("stage", [128, 2, 1024])
    ARENA_W = 34816
    arena = sb("arena", [128, ARENA_W])
    ps = st.enter_context(nc.psum_tensor("ps", [128, 4096], F32))

    def av(off, dt, shape):
        n = 1
        for s_ in shape[1:]:
            n *= s_
        nb = n * (2 if dt == BF16 else 4)
        assert off % 4 == 0 and nb % 4 == 0 and off + nb <= ARENA_W * 4, (off, nb)
        a = arena[:, off // 4:(off + nb) // 4]
        if dt == BF16:
            a = a.bitcast(BF16)
        if len(shape) > 2:
            names = "abcdefg"[:len(shape) - 1]
            pat = "p (%s) -> p %s" % (" ".join(names), " ".join(names))
            a = a.rearrange(pat, **{n_: shape[1 + k_] for k_, n_ in enumerate(names[:-1])})
        return a

    def bank(i, n=1):
        return ps[:, i * 512:(i + n) * 512]

    KB = 1024
    bufs = {}

    def B(name):
        if name not in bufs:
            bufs[name] = Buf(name.replace(".", "_").replace("[", "_").replace("]", ""))
        return bufs[name]

    PB = [B("psb%d" % i) for i in range(8)]
    ring_b = [B("ring%d" % i) for i in range(4)]
    ring_n = [0]

    def ring_load(dram_ap, shape3):
        s = ring_n[0] % 4
        ring_n[0] += 1
        a, b_ = shape3
        view = ring[:, s, 0:a * b_].rearrange("p (a b) -> p a b", a=a)
        S.dma("gpsimd", view, dram_ap, writes=[ring_b[s]], nobar=True)
        return view, ring_b[s]

    def wblock(w_d, c0, ncols, kcs=8):
        return w_d[:, c0:c0 + ncols].rearrange("(c p) n -> p c n", p=128), (kcs, ncols)

    S.dma("sync", identf[:], ident_d, writes=[B("identf")])
    S.dma("sync", masks[:], masks_d.rearrange("m p n -> p m n"), writes=[B("masks")])
    S.dma("sync", ppc[:], ppc_d, writes=[B("ppc")])
    S.dma("sync", rowc[:], rowc_d[GFIN:NR].partition_broadcast(128), writes=[B("rowc")])
    cT = av(0, F32, [128, 16])
    S.dma("sync", cT, cT_d, writes=[B("cT")])
    bg_bc = av(1 * KB, F32, [128, 2, 1024])
    S.dma("sync", bg_bc[:, 0, :], rowc_d[BGM:BGM + 1024].partition_broadcast(128), writes=[B("bg0")])
    S.dma("sync", bg_bc[:, 1, :], rowc_d[BGF:BGF + 1024].partition_broadcast(128), writes=[B("bg1")])
    S.dma("gpsimd", wdt[:], w_in_d[:, 3072:3088].rearrange("(c p) n -> p c n", p=128), writes=[B("wdt")], nobar=True)
    S.op("vector", lambda e: e.tensor_copy(out=identb[:], in_=identf[:]), reads=[B("identf")], writes=[B("identb")])
    S.op("scalar", lambda e: e.activation(out=scT[:].rearrange("p a b -> p (a b)"), in_=cT, func=AF.Silu),
         reads=[B("cT")], writes=[B("scT")])
    scB = av(10 * KB, BF16, [128, 8, 2, 128])
    S.op("vector", lambda e: e.tensor_copy(out=scB, in_=scT[:].unsqueeze(3).to_broadcast([128, 8, 2, 128])),
         reads=[B("scT")], writes=[B("scB")])
    S.op("scalar", lambda e: e.activation(out=a_bc[:], in_=alog_bc, func=AF.Exp), reads=[B("rowc")], writes=[B("a_bc")])
    S.op("vector", lambda e: e.tensor_scalar(out=a_bc[:], in0=a_bc[:], scalar1=-1.0, scalar2=None, op0=ALU.mult),
         reads=[B("a_bc")], writes=[B("a_bc")])

    modps = bank(7)[:, 0:64].rearrange("p (a b) -> p a b", a=32)
    first_mod = [True]

    def ada_blocks(lo, hi):
        for blk in range(lo, hi):
            wv, wb = ring_load(*wblock(w_ada_d, blk * 512, 512))
            mod = blk // 2
            if mod in (2, 5):
                g = 0 if mod == 2 else 1
                for j in range(2):
                    pb = 4 + (blk * 2 + j) % 3
                    for kc in range(8):
                        S.op("tensor", lambda e, pb=pb, j=j, kc=kc, wv=wv: e.matmul(
                            out=bank(pb), lhsT=scB[:, kc, j, :], rhs=wv[:, kc, :], start=(kc == 0), stop=(kc == 7)),
                            reads=[B("scB"), wb], writes=[PB[pb]])
                    half = blk % 2
                    S.op("vector", lambda e, pb=pb, g=g, j=j, half=half: e.tensor_tensor(
                        out=gate_bc[:, g, j, half * 512:(half + 1) * 512], in0=bank(pb),
                        in1=bg_bc[:, g, half * 512:(half + 1) * 512], op=ALU.add),
                        reads=[PB[pb], B("bg%d" % g)], writes=[B("gate_bc")])
            else:
                kind = {0: 0, 1: 1, 3: 2, 4: 3}[mod]
                for m in range(4):
                    idx = kind * 8 + (blk % 2) * 4 + m
                    for kc in range(8):
                        fm = first_mod[0]
                        first_mod[0] = False
                        S.op("tensor", lambda e, idx=idx, kc=kc, m=m, wv=wv, fm=fm: e.matmul(
                            out=modps[:, idx, :], lhsT=wv[:, kc, m * 128:(m + 1) * 128], rhs=scT[:, kc, :],
                            start=fm, stop=(kc == 7), skip_group_check=True),
                            reads=[B("scT"), wb], writes=[PB[7]])

    def ada_mods(q):
        k0 = 2 * q
        S.op("vector", lambda e: e.tensor_tensor(
            out=modpp[:, k0:k0 + 2, :, :].rearrange("p a b c -> p (a b) c"), in0=modps[:, k0 * 8:(k0 + 2) * 8, :],
            in1=ppc[:, BSHM + k0 * 8:BSHM + (k0 + 2) * 8].unsqueeze(2).to_broadcast([128, 16, 2]), op=ALU.add),
            reads=[PB[7], B("ppc")], writes=[B("modpp%d" % q)])
        kind, goff = ((1, G1), (3, G2))[q]
        S.op("vector", lambda e: e.scalar_tensor_tensor(
            out=gsc[:, q, :, :], in0=modpp[:, kind, :, :], scalar=1.0,
            in1=ppc[:, goff:goff + 8].unsqueeze(2).to_broadcast([128, 8, 2]), op0=ALU.add, op1=ALU.mult),
            reads=[B("modpp%d" % q), B("ppc")], writes=[B("gsc%d" % q)])

    sm_n = [0]

    def small_slot():
        i = sm_n[0] % 32
        sm_n[0] += 1
        return small[:, 2 * i:2 * i + 1], small[:, 2 * i + 1:2 * i + 2], B("sm%d" % i)

    def rms_scale(src_ap, src_bufs, width, junk_ap, junk_buf):
        ss, rr, sbuf_ = small_slot()
        S.op("scalar", lambda e: e.activation(out=junk_ap, in_=src_ap, func=AF.Square, accum_out=ss),
             reads=src_bufs, writes=[junk_buf, sbuf_])
        S.op("scalar", lambda e: e.activation(out=rr, in_=ss, func=AF.Ln, scale=1.0 / width, bias=EPS),
             reads=[sbuf_], writes=[sbuf_])
        S.op("scalar", lambda e: e.activation(out=rr, in_=rr, func=AF.Exp, scale=-0.5), reads=[sbuf_], writes=[sbuf_])
        return rr, sbuf_

    A_X = 0
    A_HT = 32 * KB
    A_CAT = 48 * KB
    A_P = 64 * KB

    def norm_to_hT(tb, which, hT, src_tiles, scratch=0, driver=None):
        j = tb
        q = 0 if which == "m" else 1
        shk = 0 if which == "m" else 2
        junk = av(A_P + scratch + 0 * KB, BF16, [128, 1024])
        def tile_gen(ip):
            tiles = (2 * ip, 2 * ip + 1)
            srcs = [src_tiles(i) for i in tiles]
            rrs = [rms_scale(src_ap, [src_b], 1024, junk, B("junk")) for (src_ap, src_b) in srcs]
            yield
            xns = []
            for (i, (src_ap, src_b), (rr, rb)) in zip(tiles, srcs, rrs):
                xn = av(A_P + scratch + 2 * KB + (i % 2) * 2 * KB, BF16, [128, 1024])
                xnb = B("xn%d" % (i % 2))
                S.op("scalar", lambda e, xn=xn, src_ap=src_ap, rr=rr: e.activation(out=xn, in_=src_ap, func=AF.Copy, scale=rr),
                     reads=[src_b, rb], writes=[xnb])
                xns.append((xn, xnb))
            yield
            pb0 = 2 * (ip % 2)
            for t, (xn, xnb) in enumerate(xns):
                pT = bank(pb0 + t).bitcast(BF16)
                for kc in range(8):
                    S.op("tensor", lambda e, pT=pT, xn=xn, kc=kc: e.transpose(
                        out=pT[:, kc * 128:(kc + 1) * 128], in_=xn[:, kc * 128:(kc + 1) * 128], identity=identb[:]),
                        reads=[xnb, B("identb")], writes=[PB[pb0 + t]])
            yield
            pT2 = bank(pb0, 2).bitcast(BF16).rearrange("p (t f) -> p t f", t=2)
            for kc in range(8):
                S.op("vector", lambda e, kc=kc: e.tensor_scalar(
                    out=hT[:, kc, 2 * ip * 128:(2 * ip + 2) * 128].rearrange("p (t c) -> p t c", t=2),
                    in0=pT2[:, :, kc * 128:(kc + 1) * 128],
                    scalar1=gsc[:, q, kc, j:j + 1], scalar2=modpp[:, shk, kc, j:j + 1], op0=ALU.mult, op1=ALU.add),
                    reads=[PB[pb0], PB[pb0 + 1], B("gsc%d" % q), B("modpp%d" % q)], writes=[B("hT")])
        if driver is not None:
            driver(tile_gen)
        else:
            run_pipe((tile_gen(ip) for ip in range(4)), depth=2, stagger=2)

    def proj_fm(wv, wb, col, hT, pbase, rbuf):
        for half in range(2):
            for kc in range(8):
                S.op("tensor", lambda e, half=half, kc=kc: e.matmul(
                    out=bank(pbase + half), lhsT=wv[:, kc, col:col + 128], rhs=hT[:, kc, half * 512:(half + 1) * 512],
                    start=(kc == 0), stop=(kc == 7)),
                    reads=[wb, rbuf], writes=[PB[pbase + half]])

    def process_tb(tb):
        j = tb
        nseq, L = (1, 1024) if tb == 0 else (4, 256)
        LP = L + 2
        hT = av(A_HT, BF16, [128, 8, 1024])
        catT = av(A_CAT, BF16, [128, 8, 1024])
        X = av(A_X, F32, [128, 8, 1024])

        n1_off = 0 if tb == 0 else 44 * KB
        xring = av(A_P + n1_off + 8 * KB, F32, [128, 2, 1024])

        def src_x(i):
            xb = B("xring%d" % (i % 2))
            S.dma("sync", xring[:, i % 2, :], x_d[(tb * 8 + i) * 128:(tb * 8 + i + 1) * 128, :], writes=[xb])
            return xring[:, i % 2, :], xb
        if tb == 0:
            norm_to_hT(tb, "m", hT, src_x, scratch=n1_off)
        else:
            norm_to_hT(tb, "m", hT, src_x, scratch=n1_off, driver=n1_deferred.append)
        yield

        BT = av(A_P + 0 * KB, BF16, [128, 2, 1024])
        CT = av(A_P + 4 * KB, BF16, [128, 2, 1024])
        xtok = av(A_P + 8 * KB, BF16, [128, 8, 768])
        rawp = av(A_P + 20 * KB, F32, [128, 2, nseq * LP])
        accb = av(A_P + 30 * KB, F32, [128, 2, 1024])
        xsT = av(A_P + 55 * KB, BF16, [128, 4, 1024])
        for r in range(2):
            rp3 = rawp[:, r, :].rearrange("p (s l) -> p s l", s=nseq)
            S.op("vector", lambda e, rp3=rp3: e.memset(rp3[:, :, 0:1], 0.0), writes=[B("rawp%d" % r)])
            S.op("vector", lambda e, rp3=rp3: e.memset(rp3[:, :, LP - 1:LP], 0.0), writes=[B("rawp%d" % r)])
        wx = [ring_load(*wblock(w_in_d, 2048 + 512 * q, 512)) for q in range(2)]
        wz = ring_load(*wblock(w_in_d, 1536, 512))

        def xbc_proj(jc):
            wv, wb = wx[jc // 4]
            proj_fm(wv, wb, (jc % 4) * 128, hT, 2 * (jc % 4), B("hT"))

        def xbc_dst(jc):
            if jc < 4:
                return xsT[:, jc, :], B("xsT%d" % jc)
            if jc < 6:
                return BT[:, jc - 4, :], B("BT%d" % (jc - 4))
            return CT[:, jc - 6, :], B("CT")

        def xbc_silu(jc):
            dst, db = xbc_dst(jc)
            S.op("scalar", lambda e: e.activation(out=dst, in_=accb[:, jc % 2, :], func=AF.Silu),
                 reads=[B("acc%d" % (jc % 2))], writes=[db])
        xbc_proj(0)
        xbc_proj(1)
        for jc in range(8):
            r = jc % 2
            pbase = 2 * (jc % 4)
            rp3 = rawp[:, r, :].rearrange("p (s l) -> p s l", s=nseq)
            rb = B("rawp%d" % r)
            ab = B("acc%d" % r)
            acc3 = accb[:, r, :].rearrange("p (s l) -> p s l", s=nseq)
            S.op("scalar", lambda e: e.activation(
                out=rp3[:, :, 1:L + 1], in_=bank(pbase, 2).rearrange("p (s l) -> p s l", s=nseq), func=AF.Copy),
                reads=[PB[pbase], PB[pbase + 1]], writes=[rb])
            if jc + 2 < 8:
                xbc_proj(jc + 2)
            if jc >= 1:
                xbc_silu(jc - 1)
            S.op("vector", lambda e: e.tensor_scalar(
                out=acc3, in0=rp3[:, :, 1:L + 1], scalar1=ppc[:, WCX + jc * 3 + 1:WCX + jc * 3 + 2],
                scalar2=ppc[:, BCX + jc:BCX + jc + 1], op0=ALU.mult, op1=ALU.add),
                reads=[rb, B("ppc")], writes=[ab])
            for k in (0, 2):
                S.op("vector", lambda e, k=k: e.scalar_tensor_tensor(
                    out=acc3, in0=rp3[:, :, k:k + L], scalar=ppc[:, WCX + jc * 3 + k:WCX + jc * 3 + k + 1],
                    in1=acc3, op0=ALU.mult, op1=ALU.add),
                    reads=[rb, ab, B("ppc")], writes=[ab])
        xbc_silu(7)
        for jc in range(6):
            dst, db = xbc_dst(jc)
            tb_ = jc % 2
            pT = bank(tb_).bitcast(BF16)
            for i in range(8):
                S.op("tensor", lambda e, i=i: e.transpose(
                    out=pT[:, i * 128:(i + 1) * 128], in_=dst[:, i * 128:(i + 1) * 128], identity=identb[:]),
                    reads=[db, B("identb")], writes=[PB[tb_]])
            S.op("scalar", lambda e: e.activation(
                out=xtok[:, :, jc * 128:(jc + 1) * 128], in_=pT.rearrange("p (i c) -> p i c", i=8), func=AF.Copy),
                reads=[PB[tb_]], writes=[B("xtok")])

        dtp = bank(6)[:, 0:128].rearrange("p (i c) -> p i c", i=8)
        for i in range(8):
            for kc in range(8):
                S.op("tensor", lambda e, i=i, kc=kc: e.matmul(
                    out=dtp[:, i, :], lhsT=hT[:, kc, i * 128:(i + 1) * 128], rhs=wdt[:, kc, :],
                    start=(i == 0 and kc == 0), stop=(kc == 7), skip_group_check=True),
                    reads=[B("hT"), B("wdt")], writes=[PB[6]])
        dsm = av(A_P + 42 * KB, F32, [128, 6, 8, 16])
        xs_, tmp_, dt_, da_, dtdte_ = dsm[:, 0], dsm[:, 1], dsm[:, 2], dsm[:, 3], dsm[:, 4]
        S.op("vector", lambda e: e.tensor_tensor(out=xs_, in0=dtp, in1=dtb_bc.unsqueeze(1).to_broadcast([128, 8, 16]), op=ALU.add),
             reads=[PB[6], B("rowc")], writes=[B("dsm")])
        S.op("scalar", lambda e: e.activation(out=tmp_, in_=xs_, func=AF.Abs),
             reads=[B("dsm")], writes=[B("dsm")])
        S.op("scalar", lambda e: e.activation(out=tmp_, in_=tmp_, func=AF.Exp, scale=-1.0), reads=[B("dsm")], writes=[B("dsm")])
        S.op("scalar", lambda e: e.activation(out=tmp_, in_=tmp_, func=AF.Ln, bias=1.0), reads=[B("dsm")], writes=[B("dsm")])
        S.op("vector", lambda e: e.scalar_tensor_tensor(out=dt_, in0=xs_, scalar=0.0, in1=tmp_, op0=ALU.max, op1=ALU.add),
             reads=[B("dsm")], writes=[B("dsm")])
        S.op("vector", lambda e: e.tensor_tensor(out=da_, in0=dt_, in1=a_bc[:].unsqueeze(1).to_broadcast([128, 8, 16]), op=ALU.mult),
             reads=[B("dsm"), B("a_bc")], writes=[B("dsm")])
        csp = bank(7)[:, 0:384].rearrange("p (n f) -> p n f", n=6)
        specs = [(0, T_LE), (8, T_GE), (0, M_GT), (8, M_LT), (0, ONES), (8, ONES)]
        for n_, (c_, msk) in enumerate(specs):
            S.op("tensor", lambda e, c_=c_, msk=msk, n_=n_: e.matmul(
                out=csp[:, n_, :], lhsT=msk, rhs=da_[:, :, c_:c_ + 8],
                start=(n_ == 0), stop=True, skip_group_check=True),
                reads=[B("dsm"), B("masks")], writes=[PB[7]])
        expall = av(A_P + 45 * KB, F32, [128, 8, 48])
        S.op("scalar", lambda e: e.activation(
            out=expall.rearrange("p i (n h) -> p n i h", n=6), in_=csp.rearrange("p n (i h) -> p n i h", i=8), func=AF.Exp),
            reads=[PB[7]], writes=[B("expall")])
        S.op("vector", lambda e: e.tensor_tensor(out=dtdte_, in0=expall[:, :, 16:32], in1=dt_, op=ALU.mult),
             reads=[B("expall"), B("dsm")], writes=[B("dsm2")])

        S.barrier()
        T0 = A_X
        Lb = av(T0 + 0 * KB, F32, [128, 2, 8, 128])
        E32 = av(T0 + 8 * KB, F32, [128, 2, 512])
        Gm = av(T0 + 12 * KB, F32, [128, 2, 2, 128])
        scb = av(T0 + 14 * KB, BF16, [128, 2, 2, 8, 128])
        xw = av(T0 + 22 * KB, BF16, [128, 2, 3, 512])
        xdc = av(T0 + 28 * KB, BF16, [128, 2, 512])
        S32 = av(A_P + 38 * KB, F32, [128, 2, 512])
        startb = av(A_P + 47 * KB, BF16, [128, 8, 512])
        yv = av(A_P + 55 * KB, F32, [128, 512])
        sz = av(A_P + 57 * KB, F32, [128, 512])
        yg = av(A_P + 59 * KB, F32, [128, 512])
        ynt = av(A_P + 61 * KB, BF16, [128, 512])
        junk2 = av(A_P + 62 * KB, BF16, [128, 512])
        stin = av(A_P + 63 * KB, F32, [128, 4, 128])
        Sfb = av(A_P + 65 * KB, BF16, [128, 512])
        o12 = av(A_P + 66 * KB, F32, [128, 2, 512])
        wzv, wzb = wz
        wA = [ring_load(*wblock(w_in_d, 512 * q, 512)) for q in range(3)]
        PB0a, PB0b = PB[0], PB[7]

        def bc_h(ap8):
            return ap8.unsqueeze(2).to_broadcast([128, 8, 64])

        def x3(ap512):
            return ap512.rearrange("p (h q) -> p h q", h=8)

        def load_state(src_d, d):
            S.dma("sync", stin, src_d.rearrange("(q p) n -> p q n", p=128), writes=[B("stin")])
            for q in range(4):
                S.op("tensor", lambda e, q=q: e.transpose(out=bank(5)[:, q * 128:(q + 1) * 128], in_=stin[:, q, :], identity=identf[:]),
                     reads=[B("stin"), B("identf")], writes=[PB[5]])
            S.op("scalar", lambda e, d=d: e.activation(out=S32[:, d, :], in_=bank(5), func=AF.Copy),
                 reads=[PB[5]], writes=[B("S32_%d" % d)])

        def store_state(dst_d, d, slot):
            for q in range(4):
                S.op("tensor", lambda e, q=q, d=d: e.transpose(out=bank(5)[:, q * 128:(q + 1) * 128], in_=S32[:, d, q * 128:(q + 1) * 128], identity=identf[:]),
                     reads=[B("S32_%d" % d), B("identf")], writes=[PB[5]])
            sv = stage[:, slot, 0:512].rearrange("p (q n) -> p q n", q=4)
            S.op("scalar", lambda e, sv=sv: e.activation(out=sv, in_=bank(5).rearrange("p (q n) -> p q n", q=4), func=AF.Copy),
                 reads=[PB[5]], writes=[B("stage%d" % slot)])
            S.dma("sync", dst_d.rearrange("(q p) n -> p q n", p=128), sv, reads=[B("stage%d" % slot)], final=True)

        def state_update(c, d, zero):
            col = 8 * d
            xdec = xdc[:, d, :]
            S.op("gpsimd", lambda e: e.tensor_tensor(out=x3(xdec), in0=x3(xtok[:, c, 0:512]), in1=bc_h(dtdte_[:, c, col:col + 8]), op=ALU.mult),
                 reads=[B("xtok"), B("dsm2")], writes=[B("xdec%d" % d)])
            for g in range(2):
                S.op("tensor", lambda e, g=g: e.matmul(out=bank(6)[:, g * 256:(g + 1) * 256], lhsT=xtok[:, c, 512 + g * 128:512 + (g + 1) * 128],
                                                   rhs=xdec[:, g * 256:(g + 1) * 256], start=(g == 0), stop=True, skip_group_check=True),
                     reads=[B("xtok"), B("xdec%d" % d)], writes=[PB[6]])
            sbuf_ = B("S32_%d" % d)
            if zero:
                S.op("vector", lambda e: e.tensor_copy(out=S32[:, d, :], in_=bank(6)), reads=[PB[6]], writes=[sbuf_])
            else:
                S.op("gpsimd", lambda e: e.tensor_tensor(out=x3(S32[:, d, :]), in0=x3(S32[:, d, :]), in1=bc_h(expall[:, c, 32 + col:32 + col + 8]), op=ALU.mult),
                     reads=[sbuf_, B("expall")], writes=[sbuf_])
                S.op("vector", lambda e: e.tensor_tensor(out=S32[:, d, :], in0=S32[:, d, :], in1=bank(6), op=ALU.add),
                     reads=[sbuf_, PB[6]], writes=[sbuf_])

        def stage_a(c, sl):
            tok = slice(c * 128, (c + 1) * 128)
            for q_, sc_ap in enumerate((dt_[:, c, 0:8], dt_[:, c, 8:16], dsk_bc)):
                rds = [B("xtok"), B("dsm") if q_ < 2 else B("rowc")]
                S.op("gpsimd", lambda e, q_=q_, sc_ap=sc_ap: e.tensor_tensor(out=x3(xw[:, sl, q_, :]), in0=x3(xtok[:, c, 0:512]), in1=bc_h(sc_ap), op=ALU.mult),
                     reads=rds, writes=[B("xw%d_%d" % (sl, q_))])
            for g in range(2):
                S.op("tensor", lambda e, g=g: e.matmul(out=bank(0)[:, g * 128:(g + 1) * 128], lhsT=BT[:, g, tok], rhs=CT[:, g, tok], start=True, stop=True),
                     reads=[B("BT%d" % g), B("CT")], writes=[PB0a])
            yield
            for d, msk in ((0, T_LE), (1, T_GE)):
                S.op("vector", lambda e, d=d, msk=msk: e.tensor_tensor(
                    out=Gm[:, d, :, :], in0=bank(0)[:, 0:256].rearrange("p (g l) -> p g l", g=2),
                    in1=msk.unsqueeze(1).to_broadcast([128, 2, 128]), op=ALU.mult),
                    reads=[PB0a, B("masks")], writes=[B("Gm%d" % d)])
            for d, msk in ((0, T_LE), (1, T_GE)):
                S.op("vector", lambda e, d=d, msk=msk: e.tensor_tensor(
                    out=Lb[:, d, :, :], in0=msk.unsqueeze(1).to_broadcast([128, 8, 128]),
                    in1=da_[:, c, 8 * d:8 * d + 8].unsqueeze(2).to_broadcast([128, 8, 128]), op=ALU.mult),
                    reads=[B("masks"), B("dsm")], writes=[B("Lb%d" % d)])
            yield
            for d, tri in ((0, M_GT), (1, M_LT)):
                if d == 1:
                    yield
                for hq in range(2):
                    pb = 1 + hq
                    S.op("tensor", lambda e, d=d, tri=tri, pb=pb, hq=hq: e.matmul(
                        out=bank(pb), lhsT=tri, rhs=Lb[:, d, hq * 4:(hq + 1) * 4, :].rearrange("p h l -> p (h l)"),
                        start=True, stop=True),
                        reads=[B("Lb%d" % d), B("masks")], writes=[PB[pb]])
                    S.op("scalar", lambda e, pb=pb: e.activation(out=bank(pb), in_=bank(pb), func=AF.Exp),
                         reads=[PB[pb]], writes=[PB[pb]])
                    S.op("vector", lambda e, d=d, hq=hq, pb=pb: e.tensor_tensor(
                        out=scb[:, sl, d, hq * 4:(hq + 1) * 4, :], in0=bank(pb).rearrange("p (h l) -> p h l", h=4),
                        in1=Gm[:, d, hq, :].unsqueeze(1).to_broadcast([128, 4, 128]), op=ALU.mult),
                        reads=[PB[pb], B("Gm%d" % d)], writes=[B("scb%d_%d" % (sl, d))])

        def stage_b(c, sl, fzero, b_off_zero, do_update):
            tok = slice(c * 128, (c + 1) * 128)
            if not fzero:
                S.op("scalar", lambda e: e.activation(out=Sfb, in_=S32[:, 0, :], func=AF.Copy), reads=[B("S32_0")], writes=[B("Sfb")])
            if do_update:
                state_update(c, 0, fzero)
            yield
            n_mm = 0
            for d in range(2):
                for h in range(8):
                    S.op("tensor", lambda e, d=d, h=h, n_mm=n_mm: e.matmul(
                        out=bank(3)[:, h * 64:(h + 1) * 64], lhsT=scb[:, sl, d, h, :], rhs=xw[:, sl, d, h * 64:(h + 1) * 64],
                        start=(n_mm == 0), stop=False, skip_group_check=True),
                        reads=[B("scb%d_%d" % (sl, d)), B("xw%d_%d" % (sl, d))], writes=[PB[3]])
                    n_mm += 1
            S.op("tensor", lambda e: e.matmul(out=bank(3), lhsT=identb[:], rhs=xw[:, sl, 2, :], start=False, stop=True, skip_group_check=True),
                 reads=[B("identb"), B("xw%d_2" % sl)], writes=[PB[3]])
            yield
            terms = []
            if not fzero:
                for g in range(2):
                    S.op("tensor", lambda e, g=g: e.matmul(out=bank(4)[:, g * 256:(g + 1) * 256], lhsT=CT[:, g, tok], rhs=Sfb[:, g * 256:(g + 1) * 256],
                                                       start=(g == 0), stop=True, skip_group_check=True),
                         reads=[B("CT"), B("Sfb")], writes=[PB[4]])
                S.op("vector", lambda e: e.tensor_tensor(out=x3(o12[:, 0, :]), in0=x3(bank(4)), in1=bc_h(expall[:, c, 0:8]), op=ALU.mult),
                     reads=[PB[4], B("expall")], writes=[B("o1")])
                terms.append((o12[:, 0, :], B("o1")))
            if not b_off_zero:
                for g in range(2):
                    S.op("tensor", lambda e, g=g: e.matmul(out=bank(5)[:, g * 256:(g + 1) * 256], lhsT=CT[:, g, tok], rhs=startb[:, c, g * 256:(g + 1) * 256],
                                                       start=(g == 0), stop=True, skip_group_check=True),
                         reads=[B("CT"), B("startb")], writes=[PB[5]])
                S.op("vector", lambda e: e.tensor_tensor(out=x3(o12[:, 1, :]), in0=x3(bank(5)), in1=bc_h(expall[:, c, 8:16]), op=ALU.mult),
                     reads=[PB[5], B("expall")], writes=[B("o2")])
                terms.append((o12[:, 1, :], B("o2")))
            yield
            if len(terms) == 2:
                S.op("gpsimd", lambda e: e.tensor_tensor(out=o12[:, 0, :], in0=o12[:, 0, :], in1=o12[:, 1, :], op=ALU.add),
                     reads=[B("o1"), B("o2")], writes=[B("o1")])
                terms = [terms[0]]
            if terms:
                tap, tbuf = terms[0]
                S.op("vector", lambda e, tap=tap: e.tensor_tensor(out=yv, in0=tap, in1=bank(3), op=ALU.add),
                     reads=[tbuf, PB[3]], writes=[B("yv")])
            else:
                S.op("vector", lambda e: e.tensor_copy(out=yv, in_=bank(3)), reads=[PB[3]], writes=[B("yv")])
            yield
            for kc in range(8):
                S.op("tensor", lambda e, kc=kc: e.matmul(out=bank(7), lhsT=hT[:, kc, tok], rhs=wzv[:, kc, :], start=(kc == 0), stop=(kc == 7)),
                     reads=[B("hT"), wzb], writes=[PB[7]])
            S.op("scalar", lambda e: e.activation(out=bank(7), in_=bank(7), func=AF.Silu), reads=[PB[7]], writes=[PB[7]])
            yield
            S.op("vector", lambda e: e.tensor_tensor(out=yg, in0=yv, in1=bank(7), op=ALU.mult), reads=[B("yv"), PB[7]], writes=[B("yg")])
            yield
            rr, rb = rms_scale(yg, [B("yg")], 512, junk2, B("junk2"))
            yield
            S.op("vector", lambda e, rr=rr: e.scalar_tensor_tensor(out=ynt, in0=yg, scalar=rr, in1=gssd_bc, op0=ALU.mult, op1=ALU.mult),
                 reads=[B("yg"), rb, B("rowc")], writes=[B("ynt")])
            pT = bank(7).bitcast(BF16)[:, 0:512]
            for q in range(4):
                S.op("tensor", lambda e, q=q, pT=pT: e.transpose(out=pT[:, q * 128:(q + 1) * 128], in_=ynt[:, q * 128:(q + 1) * 128], identity=identb[:]),
                     reads=[B("ynt"), B("identb")], writes=[PB0b])
            yield
            S.op("scalar", lambda e, pT=pT: e.activation(out=catT[:, 4:8, tok], in_=pT.rearrange("p (q t) -> p q t", q=4), func=AF.Copy),
                 reads=[PB0b], writes=[B("catT")])

        CPS = L // 128
        has_init = (tb == 0)
        b_zero_at = {}

        def bwd_gen():
            for s in range(nseq):
                chunks = list(range(s * CPS, (s + 1) * CPS))
                if has_init:
                    load_state(stb_d, 1)
                bzero = not has_init
                for c in reversed(chunks):
                    b_zero_at[c] = bzero
                    if not bzero:
                        S.op("scalar", lambda e, c=c: e.activation(out=startb[:, c, :], in_=S32[:, 1, :], func=AF.Copy),
                             reads=[B("S32_1")], writes=[B("startb")])
                    state_update(c, 1, bzero)
                    bzero = False
                    yield
                if tb == 1:
                    store_state(nsb_d[s], 1, 0)
        run_pattern(stage_a(0, 0), bwd_gen(), "BABBABBABB")
        allc = list(range(8))
        fzero = True
        for idx, c in enumerate(allc):
            s = c // CPS
            first = (c % CPS == 0)
            last = (c % CPS == CPS - 1)
            if first:
                if has_init:
                    load_state(stf_d, 0)
                fzero = not has_init
            gb_ = stage_b(c, idx % 2, fzero, b_zero_at[c], do_update=not (tb == 0 and last))
            ga_ = stage_a(allc[idx + 1], (idx + 1) % 2) if idx + 1 < 8 else None
            run_pattern(ga_, gb_, SSD_PATTERN)
            fzero = False
            if last and tb == 1:
                store_state(nsf_d[s], 0, 1)

        S.barrier()
        for i in range(8):
            S.dma("sync", X[:, i, :], x_d[(tb * 8 + i) * 128:(tb * 8 + i + 1) * 128, :], writes=[B("X%d" % i)])
        tpad = av(A_P + 20 * KB, F32, [128, 2, nseq * LP])
        acc2 = av(A_P + 30 * KB, F32, [128, 2, 1024])
        for r in range(2):
            rp3 = tpad[:, r, :].rearrange("p (s l) -> p s l", s=nseq)
            S.op("vector", lambda e, rp3=rp3: e.memset(rp3[:, :, 0:1], 0.0), writes=[B("rawp%d" % r)])
            S.op("vector", lambda e, rp3=rp3: e.memset(rp3[:, :, LP - 1:LP], 0.0), writes=[B("rawp%d" % r)])

        def mixa_proj(jc, which):
            widx = (0, 2, 1)[which]
            proj_fm(wA[widx][0], wA[widx][1], jc * 128, hT, 2 * ((3 * jc + which) % 4), B("hT"))

        def mixa_pb(jc, which):
            return 2 * ((3 * jc + which) % 4)
        for w_ in range(3):
            mixa_proj(0, w_)
        for jc in range(4):
            r = jc % 2
            tp3 = tpad[:, r, :].rearrange("p (s l) -> p s l", s=nseq)
            tb3 = B("rawp%d" % r)
            ab = B("acc%d" % r)
            a3 = acc2[:, r, :].rearrange("p (s l) -> p s l", s=nseq)
            ph, pg, pq = mixa_pb(jc, 0), mixa_pb(jc, 1), mixa_pb(jc, 2)
            S.op("scalar", lambda e: e.activation(out=acc2[:, r, :], in_=bank(ph, 2), func=AF.Copy),
                 reads=[PB[ph], PB[ph + 1]], writes=[ab])
            if jc + 1 < 4:
                mixa_proj(jc + 1, 0)
                mixa_proj(jc + 1, 1)
            S.op("vector", lambda e: e.tensor_tensor(
                out=tp3[:, :, 1:L + 1], in0=a3, in1=bank(pg, 2).rearrange("p (s l) -> p s l", s=nseq), op=ALU.mult),
                reads=[ab, PB[pg], PB[pg + 1]], writes=[tb3])
            if jc + 1 < 4:
                mixa_proj(jc + 1, 2)
            S.op("vector", lambda e: e.tensor_scalar(
                out=a3, in0=tp3[:, :, 0:L], scalar1=ppc[:, WCS + jc * 3:WCS + jc * 3 + 1], scalar2=None, op0=ALU.mult),
                reads=[tb3, B("ppc")], writes=[ab])
            for k in (1, 2):
                S.op("vector", lambda e, k=k: e.scalar_tensor_tensor(
                    out=a3, in0=tp3[:, :, k:k + L], scalar=ppc[:, WCS + jc * 3 + k:WCS + jc * 3 + k + 1], in1=a3, op0=ALU.mult, op1=ALU.add),
                    reads=[tb3, ab, B("ppc")], writes=[ab])
            S.op("vector", lambda e: e.tensor_tensor(out=catT[:, jc, :], in0=acc2[:, r, :], in1=bank(pq, 2), op=ALU.mult),
                 reads=[ab, PB[pq], PB[pq + 1]], writes=[B("catT")])
        S.barrier()

        wo = [ring_load(*wblock(w_out_d, 512 * q, 512)) for q in range(2)]
        tmpo = av(A_P + 0 * KB, F32, [128, 2, 512])

        def wout_tile(i):
            for half in range(2):
                pb = 4 + (i * 2 + half) % 4
                wv, wb = wo[half]
                for kc in range(8):
                    S.op("tensor", lambda e, kc=kc: e.matmul(
                        out=bank(pb), lhsT=catT[:, kc, i * 128:(i + 1) * 128], rhs=wv[:, kc, :], start=(kc == 0), stop=(kc == 7)),
                        reads=[B("catT"), wb], writes=[PB[pb]])
                t_ = tmpo[:, pb % 2, :]
                S.op("vector", lambda e: e.tensor_tensor(
                    out=t_, in0=bank(pb), in1=gate_bc[:, 0, j, half * 512:(half + 1) * 512], op=ALU.mult),
                    reads=[PB[pb], B("gate_bc")], writes=[B("tmpo%d" % (pb % 2))])
                S.op("gpsimd", lambda e: e.tensor_tensor(
                    out=X[:, i, half * 512:(half + 1) * 512], in0=X[:, i, half * 512:(half + 1) * 512], in1=t_, op=ALU.add),
                    reads=[B("X%d" % i), B("tmpo%d" % (pb % 2))], writes=[B("X%d" % i)])

        def wout_n2_driver(tile_gen):
            active = []
            for i in range(8):
                wout_tile(i)
                if i % 2 == 1:
                    active.append(tile_gen(i // 2))
                for g in list(active):
                    try:
                        next(g)
                    except StopIteration:
                        active.remove(g)
            while active:
                for g in list(active):
                    try:
                        next(g)
                    except StopIteration:
                        active.remove(g)
        norm_to_hT(tb, "f", hT, lambda i: (X[:, i, :], B("X%d" % i)), scratch=8 * KB, driver=wout_n2_driver)
        S.barrier()

        actT = av(A_P + 0 * KB, BF16, [128, 22, 1024])
        if tb == 0:
            RW = 18 * 66
            PE_TAPS = [(0, 0), (0, 1), (0, 2)]
        else:
            RW = 4 * 258
            PE_TAPS = []
        rawf = av(A_P + 44 * KB, F32, [128, 2, RW])
        accf = av(A_P + 54 * KB, F32, [128, 2, 1024])
        dg = av(A_P + 68 * KB, F32, [128, 2, 3, 128])
        for r in range(2):
            S.op("vector", lambda e, r=r: e.memset(rawf[:, r, :], 0.0), writes=[B("rawf%d" % r)])
        pend = [None]

        def flush_pend():
            if pend[0] is not None:
                pc, pr = pend[0]
                S.op("scalar", lambda e: e.activation(out=actT[:, pc, :], in_=accf[:, pr, :], func=AF.Silu),
                     reads=[B("accfD%d" % pr)], writes=[B("actT%d" % pc)])
                pend[0] = None
        upw = {}

        def u_base(c):
            return 2 * (c % 2) if PE_TAPS else 2 * (c % 4)

        def emit_proj(c):
            blk, m = divmod(c, 4)
            if blk not in upw:
                upw[blk] = ring_load(*wblock(w_up_d, 512 * blk, 512))
            if m == 0 and blk + 1 < 11 and (blk + 1) not in upw:
                upw[blk + 1] = ring_load(*wblock(w_up_d, 512 * (blk + 1), 512))
            wv, wb = upw[blk]
            proj_fm(wv, wb, m * 128, hT, u_base(c), B("hT"))
        emit_proj(0)
        for c in range(44):
            r = c % 2
            pbase = u_base(c)
            rb = B("rawf%d" % r)
            abD = B("accfD%d" % r)
            if tb == 0:
                r3 = rawf[:, r, :].rearrange("p (a b) -> p a b", a=18)
                a3 = accf[:, r, :].rearrange("p (a b) -> p a b", a=16)
                pin = bank(pbase, 2).rearrange("p (a b) -> p a b", a=16)
                ctr = r3[:, 1:17, 1:65]
                taps = [(di * 3 + dj, r3[:, di:di + 16, dj:dj + 64]) for di in range(3) for dj in range(3)
                        if (di, dj) != (1, 1) and (di, dj) not in PE_TAPS]
            else:
                r3 = rawf[:, r, :].rearrange("p (a b) -> p a b", a=4)
                a3 = accf[:, r, :].rearrange("p (a b) -> p a b", a=4)
                pin = bank(pbase, 2).rearrange("p (a b) -> p a b", a=4)
                ctr = r3[:, :, 1:257]
                taps = [(3 + k, r3[:, :, k:k + 256]) for k in (0, 2)]
            S.op("scalar", lambda e: e.activation(out=ctr, in_=pin, func=AF.Copy),
                 reads=[PB[pbase], PB[pbase + 1]], writes=[rb])
            if tb == 0 or c >= 22:
                S.op("scalar", lambda e: e.activation(
                    out=a3, in_=pin, func=AF.Identity, scale=ppc[:, WFC + c * 9 + 4:WFC + c * 9 + 5], bias=ppc[:, BFC + c:BFC + c + 1]),
                    reads=[PB[pbase], PB[pbase + 1], B("ppc")], writes=[abD])
            else:
                S.op("vector", lambda e: e.tensor_scalar(
                    out=a3, in0=ctr, scalar1=ppc[:, WFC + c * 9 + 4:WFC + c * 9 + 5], scalar2=ppc[:, BFC + c:BFC + c + 1],
                    op0=ALU.mult, op1=ALU.add),
                    reads=[rb, B("ppc")], writes=[abD])
            for ki, (di, dj) in enumerate(PE_TAPS):
                wi = di * 3 + dj
                S.op("scalar", lambda e, ki=ki, wi=wi: e.activation(
                    out=dg[:, r, ki, :], in_=identf[:], func=AF.Copy, scale=ppc[:, WFC + c * 9 + wi:WFC + c * 9 + wi + 1]),
                    reads=[B("identf"), B("ppc")], writes=[B("dg%d" % r)])
            flush_pend()
            if c + 1 < 44:
                emit_proj(c + 1)
            if PE_TAPS:
                vb = 4 + 2 * (c % 2)
                for half in range(2):
                    for ki, (di, dj) in enumerate(PE_TAPS):
                        S.op("tensor", lambda e, half=half, ki=ki, di=di, dj=dj: e.matmul(
                            out=bank(vb + half), lhsT=dg[:, r, ki, :],
                            rhs=r3[:, half * 8 + di:half * 8 + di + 8, dj:dj + 64],
                            start=(ki == 0), stop=(ki == len(PE_TAPS) - 1)),
                            reads=[rb, B("dg%d" % r)], writes=[PB[vb + half]])
            for (wi, view) in taps:
                S.op("vector", lambda e, view=view, wi=wi: e.scalar_tensor_tensor(
                    out=a3, in0=view, scalar=ppc[:, WFC + c * 9 + wi:WFC + c * 9 + wi + 1], in1=a3, op0=ALU.mult, op1=ALU.add),
                    reads=[rb, abD, B("ppc")], writes=[abD])
            if PE_TAPS:
                S.op("vector", lambda e: e.tensor_tensor(out=accf[:, r, :], in0=accf[:, r, :], in1=bank(vb, 2), op=ALU.add),
                     reads=[abD, PB[vb], PB[vb + 1]], writes=[abD])
            if c < 22:
                pend[0] = (c, r)
            else:
                S.op("gpsimd", lambda e: e.tensor_tensor(out=actT[:, c - 22, :], in0=actT[:, c - 22, :], in1=accf[:, r, :], op=ALU.mult),
                     reads=[abD, B("actT%d" % (c - 22))], writes=[B("actT%d" % (c - 22))])
        flush_pend()
        yield

        tmpd = av(A_P + 62 * KB, F32, [128, 2, 512])
        actb = [B("actT%d" % c) for c in range(22)]
        for cb in range(8):
            wv, wb = ring_load(w_down_d[:, cb * 128:(cb + 1) * 128].rearrange("(c p) n -> p c n", p=128), (22, 128))
            for hb in range(2):
                pb = 4 + (cb * 2 + hb) % 4
                for i4 in range(4):
                    i = hb * 4 + i4
                    for kc in range(22):
                        S.op("tensor", lambda e, pb=pb, i4=i4, i=i, kc=kc, wv=wv: e.matmul(
                            out=bank(pb)[:, i4 * 128:(i4 + 1) * 128], lhsT=actT[:, kc, i * 128:(i + 1) * 128], rhs=wv[:, kc, :],
                            start=(i4 == 0 and kc == 0), stop=(kc == 21), skip_group_check=True),
                            reads=[actb[kc], wb], writes=[PB[pb]])
                t_ = tmpd[:, hb, :].rearrange("p (i n) -> p i n", i=4)
                tbuf = B("tmpd%d" % hb)
                S.op("vector", lambda e, pb=pb, t_=t_, cb=cb: e.tensor_tensor(
                    out=t_, in0=bank(pb).rearrange("p (i n) -> p i n", i=4),
                    in1=gate_bc[:, 1, j, cb * 128:(cb + 1) * 128].unsqueeze(1).to_broadcast([128, 4, 128]), op=ALU.mult),
                    reads=[PB[pb], B("gate_bc")], writes=[tbuf])
                xv = X[:, hb * 4:(hb + 1) * 4, cb * 128:(cb + 1) * 128]
                xbs = [B("X%d" % (hb * 4 + q)) for q in range(4)]
                S.op("vector", lambda e, xv=xv, t_=t_: e.tensor_tensor(out=xv, in0=xv, in1=t_, op=ALU.add),
                     reads=xbs + [tbuf], writes=xbs)
            for h_ in down_hooks:
                h_(cb)
        for h_ in down_hooks:
            h_(None)
        S.barrier()

        junk = av(A_P + 66 * KB, BF16, [128, 1024])
        for i in range(8):
            rr, rb = rms_scale(X[:, i, :], [B("X%d" % i)], 1024, junk, B("junkf"))
            slot = i % 2
            S.op("vector", lambda e, rr=rr, i=i, slot=slot: e.scalar_tensor_tensor(
                out=stage[:, slot, :], in0=X[:, i, :], scalar=rr, in1=gfin_bc, op0=ALU.mult, op1=ALU.mult),
                reads=[B("X%d" % i), rb, B("rowc")], writes=[B("stage%d" % slot)])
            S.dma("sync", y_d[(tb * 8 + i) * 128:(tb * 8 + i + 1) * 128, :], stage[:, slot, :], reads=[B("stage%d" % slot)], final=True)

    n1_deferred = []
    down_hooks = []

    def n1_hook(cb, _st={"act": []}):
        act = _st["act"]
        if cb is not None:
            if cb % 2 == 1:
                act.append(n1_deferred[0](cb // 2))
            rounds = 1
        else:
            rounds = 8
        for _ in range(rounds):
            for g_ in list(act):
                try:
                    next(g_)
                except StopIteration:
                    act.remove(g_)

    ada_blocks(0, 4)
    ada_mods(0)
    g0 = process_tb(0)
    g1 = process_tb(1)
    next(g0)
    ada_blocks(4, 12)
    ada_mods(1)
    S.barrier()
    next(g0)
    S.barrier()
    next(g1)
    down_hooks.append(n1_hook)
    for _ in g0:
        pass
    down_hooks.clear()
    for _ in g1:
        pass
    S.emit(st)
    if os.environ.get("MK_STATS"):
        print("ops/signals per engine:", S.stats)
    st.close()
    return nc


_NC_CACHE = {}


def _consts():
    k = np.arange(128)
    m_gt = (k[:, None] > k[None, :]).astype(np.float32)
    m_lt = (k[:, None] < k[None, :]).astype(np.float32)
    t_le = (k[:, None] <= k[None, :]).astype(np.float32)
    t_ge = (k[:, None] >= k[None, :]).astype(np.float32)
    ones = np.ones((128, 128), np.float32)
    return np.eye(128, dtype=np.float32), np.stack([m_gt, m_lt, t_le, t_ge, ones])


def kernel(x_prompt, x_sample, state_ssd_fwd, state_ssd_bwd, c, c_ctx, g_norm1, g_norm2, w_ada, b_ada, w_in,
           w_conv_short, w_conv_ssd, b_conv_ssd, dt_bias, a_log, d_skip, g_ssd_norm, w_out, w_up, w_ffn_conv,
           b_ffn_conv, w_down, g_final):
    f = lambda a: np.ascontiguousarray(np.asarray(a, dtype=np.float32))
    x_prompt, x_sample, c, c_ctx = f(x_prompt), f(x_sample), f(c), f(c_ctx)
    stf, stb = f(state_ssd_fwd), f(state_ssd_bwd)
    b_ada0 = f(b_ada)[0]

    def pp(v, nch):
        return f(v).reshape(nch, 128).T

    ppc = np.zeros((128, NPP), np.float32)
    ppc[:, G1:G1 + 8] = pp(f(g_norm1)[0], 8)
    ppc[:, G2:G2 + 8] = pp(f(g_norm2)[0], 8)
    ppc[:, BSHM:BSHM + 8] = pp(b_ada0[0:1024], 8)
    ppc[:, BSCM:BSCM + 8] = pp(b_ada0[1024:2048], 8)
    ppc[:, BSHF:BSHF + 8] = pp(b_ada0[3072:4096], 8)
    ppc[:, BSCF:BSCF + 8] = pp(b_ada0[4096:5120], 8)
    wcs = f(w_conv_short)[0]
    ppc[:, WCS:WCS + 12] = wcs.reshape(3, 4, 128).transpose(2, 1, 0).reshape(128, 12)
    wcx = f(w_conv_ssd)[0]
    ppc[:, WCX:WCX + 24] = wcx.reshape(3, 8, 128).transpose(2, 1, 0).reshape(128, 24)
    ppc[:, BCX:BCX + 8] = pp(f(b_conv_ssd)[0], 8)
    wfc = f(w_ffn_conv)[0].reshape(9, 5632)
    ppc[:, WFC:WFC + 396] = wfc.reshape(9, 44, 128).transpose(2, 1, 0).reshape(128, 396)
    ppc[:, BFC:BFC + 44] = pp(f(b_ffn_conv)[0], 44)
    rowc = np.concatenate([b_ada0[2048:3072], b_ada0[5120:6144], f(g_final), f(g_ssd_norm)[0], f(dt_bias)[0].reshape(16),
                           f(a_log)[0].reshape(16), f(d_skip)[0]]).astype(np.float32)
    assert rowc.shape[0] == NR
    ident, masks = _consts()
    shared = {"ppc": ppc, "rowc": rowc, "ident": ident, "masks": masks, "w_ada": f(w_ada)[0], "w_in": f(w_in)[0],
              "w_out": f(w_out)[0], "w_up": f(w_up)[0], "w_down": f(w_down)[0]}
    in_maps = []
    for b in range(NCORES):
        cv = np.stack([c[b], c_ctx])
        cT = np.ascontiguousarray(cv.reshape(2, 8, 128).transpose(2, 1, 0).reshape(128, 16))
        m = dict(shared)
        m["x"] = np.concatenate([x_sample[b], x_prompt[4 * b:4 * b + 4].reshape(1024, 1024)], axis=0)
        m["cT"] = cT
        m["stf"] = np.ascontiguousarray(stf[b, 0].reshape(512, 128))
        m["stb"] = np.ascontiguousarray(stb[b, 0].reshape(512, 128))
        in_maps.append(m)
    if _NC_CACHE.get("prep_only"):
        return in_maps
    if "nc" not in _NC_CACHE:
        _NC_CACHE["nc"] = build_nc()
    res = run_bass_kernel_spmd(_NC_CACHE["nc"], in_maps, core_ids=list(range(NCORES)))
    y_prompt = np.empty((32, 256, 1024), np.float32)
    y_sample = np.empty((8, 1024, 1024), np.float32)
    nsf = np.empty((32, 1, 8, 64, 128), np.float32)
    nsb = np.empty((32, 1, 8, 64, 128), np.float32)
    for b in range(NCORES):
        r = res.results[b]
        y = np.asarray(r["y"])
        y_sample[b] = y[0:1024]
        y_prompt[4 * b:4 * b + 4] = y[1024:2048].reshape(4, 256, 1024)
        nsf[4 * b:4 * b + 4, 0] = np.asarray(r["nsf"]).reshape(4, 8, 64, 128)
        nsb[4 * b:4 * b + 4, 0] = np.asarray(r["nsb"]).reshape(4, 8, 64, 128)
    return (y_prompt, y_sample, nsf, nsb)
```
